# Optimizing a Trainium2 kernel written in Bass

```python
import math
import jax, jax.numpy as jnp
from jax import lax
import numpy as np

D_MODEL = 1024
BATCH = 8
SEQ = 2048
DEPTH = 2

BRANCH_WIDTH = D_MODEL // 2
GLA_HEADS = 4
GLA_DV = BRANCH_WIDTH // GLA_HEADS
GLA_DK = GLA_DV // 2
GLA_LOW_RANK = 16
GLA_GATE_NORMALIZER = 16.0
GLA_CHUNK = 16
POOL_WINDOWS = (2, 4, 8, 16)
POOL_GROUPS = 4
POOL_GROUP_DIM = BRANCH_WIDTH // POOL_GROUPS
MOBA_HEADS = 4
MOBA_HEAD_DIM = BRANCH_WIDTH // MOBA_HEADS
MOBA_BLOCK = 256
MOBA_TOPK = 3
MOBA_Q_CHUNK = 16
REL_BUCKETS = 32
REL_MAX_EXACT = REL_BUCKETS // 2
REL_MAX_DIST = 128
D_FF = ((8 * D_MODEL + 767) // 768) * 256
NORM_EPS = 1e-6
NEG_INF = -1e30
IN_SPLITS = (GLA_HEADS * GLA_DK, GLA_HEADS * GLA_DK, GLA_HEADS * GLA_DV, GLA_LOW_RANK, GLA_HEADS * GLA_DV,
             BRANCH_WIDTH, MOBA_HEADS * MOBA_HEAD_DIM, MOBA_HEADS * MOBA_HEAD_DIM, MOBA_HEADS * MOBA_HEAD_DIM,
             D_MODEL, D_MODEL, D_MODEL)
IN_COLS = sum(IN_SPLITS)

kernel_name = "hybrid_gla_pool_moba_gated_block"


def rms_norm(x, w):
    x32 = x.astype(jnp.float32)
    y = x32 * lax.rsqrt(jnp.mean(x32 * x32, axis=-1, keepdims=True) + NORM_EPS)
    return (y * w.astype(jnp.float32)).astype(x.dtype)


def split_columns(proj):
    parts, off = [], 0
    for size in IN_SPLITS:
        parts.append(proj[..., off:off + size])
        off += size
    return parts


def t5_bucket(dist):
    n = jnp.maximum(dist, 0)
    large = REL_MAX_EXACT + (jnp.log(jnp.maximum(n, 1).astype(jnp.float32) / REL_MAX_EXACT)
                             / math.log(REL_MAX_DIST / REL_MAX_EXACT)
                             * (REL_BUCKETS - REL_MAX_EXACT)).astype(jnp.int32)
    large = jnp.minimum(large, REL_BUCKETS - 1)
    return jnp.where(n < REL_MAX_EXACT, n, large)


def gla_mix(q, k, v, g_lr, r, wg2, bg, norm_w):
    B_, S_ = q.shape[0], q.shape[1]
    f32 = jnp.float32
    glog = jax.nn.log_sigmoid((g_lr @ wg2).astype(f32) + bg.astype(f32)) / GLA_GATE_NORMALIZER
    N = S_ // GLA_CHUNK

    def chunks(t, d):
        return t.astype(f32).reshape(B_, N, GLA_CHUNK, GLA_HEADS, d).transpose(0, 3, 1, 2, 4)

    qc = chunks(q, GLA_DK) * (GLA_DK ** -0.5)
    kc = chunks(k, GLA_DK)
    vc = chunks(v, GLA_DV)
    b = jnp.cumsum(chunks(glog, GLA_DK), axis=3)
    causal = jnp.tril(jnp.ones((GLA_CHUNK, GLA_CHUNK), dtype=bool))
    diff = b[:, :, :, :, None, :] - b[:, :, :, None, :, :]
    decay = jnp.exp(jnp.where(causal[:, :, None], diff, -jnp.inf))
    attn = jnp.einsum('bhnid,bhnjd,bhnijd->bhnij', qc, kc, decay)
    o_intra = jnp.einsum('bhnij,bhnje->bhnie', attn, vc)
    b_last = b[:, :, :, -1]
    k_tail = kc * jnp.exp(b_last[:, :, :, None, :] - b)
    chunk_state = jnp.einsum('bhncd,bhnce->nbhde', k_tail, vc)
    chunk_decay = jnp.exp(b_last).transpose(2, 0, 1, 3)

    def step(state, inp):
        dec, cs = inp
        return dec[..., None] * state + cs, state

    init = jnp.zeros((B_, GLA_HEADS, GLA_DK, GLA_DV), f32)
    _, s_prev = lax.scan(step, init, (chunk_decay, chunk_state))
    o_inter = jnp.einsum('bhncd,nbhde->bhnce', qc * jnp.exp(b), s_prev)
    o = (o_intra + o_inter).reshape(B_, GLA_HEADS, S_, GLA_DV)
    o = rms_norm(o, norm_w)
    o = o.transpose(0, 2, 1, 3).reshape(B_, S_, GLA_HEADS * GLA_DV)
    return (o * jax.nn.silu(r.astype(f32))).astype(q.dtype)


def pool_mix(u, pool_w, pool_scale):
    B_, S_ = u.shape[0], u.shape[1]
    ug = u.astype(jnp.float32).reshape(B_, S_, POOL_GROUPS, POOL_GROUP_DIM)
    cs = jnp.cumsum(ug, axis=1)
    t = jnp.arange(S_)
    outs = []
    for gi, w in enumerate(POOL_WINDOWS):
        cg = cs[:, :, gi]
        prev = jnp.pad(cg, ((0, 0), (w, 0), (0, 0)))[:, :S_]
        cnt = jnp.minimum(t + 1, w).astype(jnp.float32)[None, :, None]
        outs.append((cg - prev) / cnt - ug[:, :, gi])
    p = jnp.stack(outs, axis=2)
    y = jnp.einsum('bsgc,gce->bsge', p, pool_w.astype(jnp.float32)).reshape(B_, S_, BRANCH_WIDTH)
    return (y * pool_scale.astype(jnp.float32)).astype(u.dtype)


def moba_mix(q, k, v, qn_w, kn_w, rel_bias):
    B_, S_ = q.shape[0], q.shape[1]
    f32 = jnp.float32

    def heads(t):
        return t.astype(f32).reshape(B_, S_, MOBA_HEADS, MOBA_HEAD_DIM).transpose(0, 2, 1, 3)

    qh = rms_norm(heads(q), qn_w)
    kh = rms_norm(heads(k), kn_w)
    vh = heads(v)
    nb = -(-S_ // MOBA_BLOCK)
    pad = nb * MOBA_BLOCK - S_
    kb = jnp.pad(kh, ((0, 0), (0, 0), (0, pad), (0, 0))).reshape(B_, MOBA_HEADS, nb, MOBA_BLOCK, MOBA_HEAD_DIM)
    vb = jnp.pad(vh, ((0, 0), (0, 0), (0, pad), (0, 0))).reshape(B_, MOBA_HEADS, nb, MOBA_BLOCK, MOBA_HEAD_DIM)
    kmean = jnp.mean(kb, axis=3)
    topk = min(MOBA_TOPK, nb)
    scale = MOBA_HEAD_DIM ** -0.5
    bias_hb = rel_bias.astype(f32).T
    bidx = jnp.arange(B_)[:, None, None, None]
    hidx = jnp.arange(MOBA_HEADS)[None, :, None, None]
    n_qc = S_ // MOBA_Q_CHUNK
    qc = qh.reshape(B_, MOBA_HEADS, n_qc, MOBA_Q_CHUNK, MOBA_HEAD_DIM).transpose(2, 0, 1, 3, 4)
    blk_ar = jnp.arange(MOBA_BLOCK)

    def chunk(args):
        ci, qq = args
        start = ci * MOBA_Q_CHUNK
        pos = start + jnp.arange(MOBA_Q_CHUNK)
        qblk = start // MOBA_BLOCK
        scores = jnp.einsum('bhqd,bhnd->bhqn', qq, kmean)
        scores = jnp.where(jnp.arange(nb) < qblk, scores, -jnp.inf)
        _, sel = lax.top_k(scores, topk)
        valid = sel < qblk
        kg = kb[bidx, hidx, sel]
        vg = vb[bidx, hidx, sel]
        kpos = sel[..., None] * MOBA_BLOCK + blk_ar
        bias_sel = bias_hb[hidx[..., None], t5_bucket(pos[None, None, :, None, None] - kpos)]
        lg_sel = jnp.einsum('bhqd,bhqkjd->bhqkj', qq, kg) * scale + bias_sel
        lg_sel = jnp.where(valid[..., None], lg_sel, NEG_INF)
        k_own = lax.dynamic_index_in_dim(kb, qblk, axis=2, keepdims=False)
        v_own = lax.dynamic_index_in_dim(vb, qblk, axis=2, keepdims=False)
        dist = pos[:, None] - (qblk * MOBA_BLOCK + blk_ar)[None, :]
        bias_own = bias_hb[:, t5_bucket(dist)]
        lg_own = jnp.einsum('bhqd,bhjd->bhqj', qq, k_own) * scale + bias_own
        lg_own = jnp.where(dist >= 0, lg_own, NEG_INF)
        lg = jnp.concatenate([lg_sel.reshape(B_, MOBA_HEADS, MOBA_Q_CHUNK, topk * MOBA_BLOCK), lg_own], axis=-1)
        p = jax.nn.softmax(lg, axis=-1)
        p_sel = p[..., :topk * MOBA_BLOCK].reshape(B_, MOBA_HEADS, MOBA_Q_CHUNK, topk, MOBA_BLOCK)
        p_own = p[..., topk * MOBA_BLOCK:]
        return (jnp.einsum('bhqkj,bhqkjd->bhqd', p_sel, vg)
                + jnp.einsum('bhqj,bhjd->bhqd', p_own, v_own))

    out = lax.map(chunk, (jnp.arange(n_qc, dtype=jnp.int32), qc))
    out = out.transpose(1, 0, 3, 2, 4).reshape(B_, S_, MOBA_HEADS * MOBA_HEAD_DIM)
    return out.astype(q.dtype)


def setup_inputs(seed: int = 0) -> dict:
    key = jax.random.key(seed)
    ks = jax.random.split(key, 20)
    nrm = jax.random.normal
    res_scale = (2 * DEPTH) ** -0.5
    return {
        "x": nrm(ks[0], (BATCH, SEQ, D_MODEL), jnp.float32),
        "norm1_w": 1.0 + 0.02 * nrm(ks[1], (DEPTH, D_MODEL), jnp.float32),
        "w_in": nrm(ks[2], (DEPTH, D_MODEL, IN_COLS), jnp.float32) * D_MODEL ** -0.5,
        "gla_wg2": nrm(ks[3], (DEPTH, GLA_LOW_RANK, GLA_HEADS * GLA_DK), jnp.float32) * GLA_LOW_RANK ** -0.5,
        "gla_bg": 0.1 * nrm(ks[4], (DEPTH, GLA_HEADS * GLA_DK), jnp.float32),
        "gla_norm_w": 1.0 + 0.02 * nrm(ks[5], (DEPTH, GLA_DV), jnp.float32),
        "pool_w": nrm(ks[6], (DEPTH, POOL_GROUPS, POOL_GROUP_DIM, POOL_GROUP_DIM), jnp.float32) * POOL_GROUP_DIM ** -0.5,
        "pool_scale": 1.0 + 0.02 * nrm(ks[7], (DEPTH, BRANCH_WIDTH), jnp.float32),
        "moba_qn_w": 1.0 + 0.02 * nrm(ks[8], (DEPTH, MOBA_HEAD_DIM), jnp.float32),
        "moba_kn_w": 1.0 + 0.02 * nrm(ks[9], (DEPTH, MOBA_HEAD_DIM), jnp.float32),
        "rel_bias": 0.5 * nrm(ks[10], (REL_BUCKETS, MOBA_HEADS), jnp.float32),
        "w_up_a": nrm(ks[11], (DEPTH, GLA_HEADS * GLA_DV, D_MODEL), jnp.float32) * (GLA_HEADS * GLA_DV) ** -0.5,
        "w_up_b": nrm(ks[12], (DEPTH, BRANCH_WIDTH, D_MODEL), jnp.float32) * BRANCH_WIDTH ** -0.5,
        "w_up_c": nrm(ks[13], (DEPTH, MOBA_HEADS * MOBA_HEAD_DIM, D_MODEL), jnp.float32) * (MOBA_HEADS * MOBA_HEAD_DIM) ** -0.5,
        "w_out": nrm(ks[14], (DEPTH, D_MODEL, D_MODEL), jnp.float32) * D_MODEL ** -0.5 * res_scale,
        "norm2_w": 1.0 + 0.02 * nrm(ks[15], (DEPTH, D_MODEL), jnp.float32),
        "ffn_w_gate": nrm(ks[16], (DEPTH, D_MODEL, D_FF), jnp.float32) * D_MODEL ** -0.5,
        "ffn_w_up": nrm(ks[17], (DEPTH, D_MODEL, D_FF), jnp.float32) * D_MODEL ** -0.5,
        "ffn_w_down": nrm(ks[18], (DEPTH, D_FF, D_MODEL), jnp.float32) * D_FF ** -0.5 * res_scale,
    }


def reference(x, norm1_w, w_in, gla_wg2, gla_bg, gla_norm_w, pool_w, pool_scale, moba_qn_w, moba_kn_w,
              rel_bias, w_up_a, w_up_b, w_up_c, w_out, norm2_w, ffn_w_gate, ffn_w_up, ffn_w_down):
    for l in range(DEPTH):
        h = rms_norm(x, norm1_w[l])
        (gq, gk, gv, g_lr, gr, pu, mq, mk, mv, gate_a, gate_b, gate_c) = split_columns(h @ w_in[l])
        ya = gla_mix(gq, gk, gv, g_lr, gr, gla_wg2[l], gla_bg[l], gla_norm_w[l])
        yb = pool_mix(pu, pool_w[l], pool_scale[l])
        yc = moba_mix(mq, mk, mv, moba_qn_w[l], moba_kn_w[l], rel_bias)
        merged = (jax.nn.sigmoid(gate_a) * (ya @ w_up_a[l])
                  + jax.nn.sigmoid(gate_b) * (yb @ w_up_b[l])
                  + jax.nn.sigmoid(gate_c) * (yc @ w_up_c[l]))
        x = x + merged @ w_out[l]
        h2 = rms_norm(x, norm2_w[l])
        x = x + (jax.nn.silu(h2 @ ffn_w_gate[l]) * (h2 @ ffn_w_up[l])) @ ffn_w_down[l]
    return x
```

```python
import numpy as np
import concourse.bass as bass
import concourse.mybir as mybir
from concourse.bass_utils import run_bass_kernel_spmd

F32 = mybir.dt.float32
BF16 = mybir.dt.bfloat16
AF = mybir.ActivationFunctionType
ALU = mybir.AluOpType
AX = mybir.AxisListType


class _Rec:
    def __getattr__(self, name):
        def f(*a, **kw):
            return (name, a, kw)
        return f


_REC = _Rec()


class Sched:
    ENGS = ('pe', 'act', 'dve', 'pool', 'sp')

    def __init__(self, nc):
        self.nc = nc
        self.prog = {e: [] for e in self.ENGS}
        self.cnt = {e: 0 for e in self.ENGS}
        self.esem = {e: nc.alloc_semaphore(f"es_{e}") for e in ('pe', 'act', 'dve', 'pool')}
        self.semeng = {id(s): e for e, s in self.esem.items()}
        self.waited = {e: {} for e in self.ENGS}
        self.lastw = {}
        self.readers = {}
        self.groupw = {}
        self.groupr = {}
        self.dsem = {}
        self.dcnt = {}
        self.sb_off = 16512
        self.sb_lim = 229344
        self.nps = 0
        self.nwaits = 0

    def sb(self, name, shape, dtype, at=None):
        esz = 4 if dtype == F32 else 2
        n = 1
        for s in shape[1:]:
            n *= s
        nbytes = (n * esz + 31) // 32 * 32
        if at is None:
            at = self.sb_off
            self.sb_off += nbytes
            assert self.sb_off <= self.sb_lim, (name, self.sb_off)
        return self.nc.alloc_sbuf_tensor_at(name, list(shape), dtype, offset=at)

    def ps(self, name, shape=(128, 512), dtype=F32):
        return self.nc.alloc_psum_tensor(name, list(shape), dtype)

    @staticmethod
    def _grp(key):
        return key[0] if isinstance(key, tuple) else key

    def _deps(self, eng, reads, writes):
        deps = {}

        def add(ev, kind):
            sem, val, e2 = ev
            if e2 == eng and (eng == 'pe' or kind == 'war'):
                return
            k = id(sem)
            if k not in deps or deps[k][1] < val:
                deps[k] = (sem, val)

        for key in reads:
            if key in self.lastw:
                add(self.lastw[key], 'raw')
            if not isinstance(key, tuple):
                for ev in self.groupw.get(key, {}).values():
                    add(ev, 'raw')
            else:
                g = key[0]
                if g in self.lastw:
                    add(self.lastw[g], 'raw')
        for key in writes:
            g = self._grp(key)
            if key in self.lastw:
                add(self.lastw[key], 'waw')
            for ev in self.readers.get(key, {}).values():
                add(ev, 'war')
            if isinstance(key, tuple):
                if g in self.lastw:
                    add(self.lastw[g], 'waw')
                for ev in self.readers.get(g, {}).values():
                    add(ev, 'war')
            else:
                for ev in self.groupw.get(g, {}).values():
                    add(ev, 'waw')
                for ev in self.groupr.get(g, {}).values():
                    add(ev, 'war')
        out = []
        w = self.waited[eng]
        for k, (sem, val) in deps.items():
            if w.get(k, 0) >= val:
                continue
            w[k] = val
            out.append((sem, val))
        return out

    def _commit(self, ev, reads, writes):
        k = id(ev[0])
        for key in reads:
            self.readers.setdefault(key, {})[k] = ev
            if isinstance(key, tuple):
                self.groupr.setdefault(key[0], {})[k] = ev
        for key in writes:
            self.lastw[key] = ev
            self.readers[key] = {}
            if isinstance(key, tuple):
                self.groupw.setdefault(key[0], {})[k] = ev
            else:
                self.groupw[key] = {}
                self.groupr[key] = {}

    def op(self, eng, fn, reads=(), writes=()):
        for sem, val in self._deps(eng, reads, writes):
            self.prog[eng].append(('w', sem, val))
            self.nwaits += 1
        self.cnt[eng] += 1
        sem = self.esem[eng]
        self.prog[eng].append(('o', fn(_REC), sem, 1))
        self._commit((sem, self.cnt[eng], eng), reads, writes)

    def dma(self, q, out, in_, reads=(), writes=()):
        assert len(writes) == 1
        dk = writes[0]
        for sem, val in self._deps(q, reads, writes):
            self.prog[q].append(('w', sem, val))
            self.nwaits += 1
        if dk not in self.dsem:
            self.dsem[dk] = self.nc.alloc_semaphore(f"ds{len(self.dsem)}")
            self.dcnt[dk] = 0
        self.dcnt[dk] += 16
        sem = self.dsem[dk]
        self.prog[q].append(('o', ('dma_start', (), dict(out=out, in_=in_)), sem, 16))
        self._commit((sem, self.dcnt[dk], None), reads, writes)

    def finish(self, outkeys):
        for dk in outkeys:
            sem, val = self.dsem[dk], self.dcnt[dk]
            self.prog['sp'].append(('w', sem, val))
        nc = self.nc
        prog = self.prog

        def replay(eng, items):
            for it in items:
                if it[0] == 'w':
                    eng.wait_ge(it[1], it[2])
                else:
                    name, a, kw = it[1]
                    ins = getattr(eng, name)(*a, **kw)
                    ins.then_inc(it[2], it[3])

        with nc.Block() as block:
            @block.tensor
            def _(e):
                replay(e, prog['pe'])

            @block.scalar
            def _(e):
                replay(e, prog['act'])

            @block.vector
            def _(e):
                replay(e, prog['dve'])

            @block.gpsimd
            def _(e):
                replay(e, prog['pool'])

            @block.sync
            def _(e):
                replay(e, prog['sp'])

    def barrier(self):
        for e in ('act', 'dve', 'sp'):
            w = self.waited[e]
            for e2, sem in self.esem.items():
                val = self.cnt[e2]
                if val > 0 and w.get(id(sem), 0) < val:
                    w[id(sem)] = val
                    self.prog[e].append(('w', sem, val))
            for dk, sem in self.dsem.items():
                if isinstance(dk, str) and dk.startswith('wb'):
                    continue
                val = self.dcnt[dk]
                if w.get(id(sem), 0) < val:
                    w[id(sem)] = val
                    self.prog[e].append(('w', sem, val))


S_LEN = 2048
D = 1024
NT = 16
IN_COLS = 6672
DFF = 2816
KFF = 22
EPS = 1e-6
C_GQ, C_GK, C_GV, C_LR, C_GR, C_PU, C_MQ, C_MK, C_MV, C_GA, C_GB, C_GC = (
    0, 256, 512, 1024, 1040, 1552, 2064, 2576, 3088, 3600, 4624, 5648)
POOL_W = (2, 4, 8, 16)
NEG = -30000.0
SB_BASE = 16512


class _Stop(Exception):
    pass


def build_program(layers, n_layers_total=2, debug=None, stop=None):
    marks = []

    def chk(name):
        marks.append((name, dict(S.cnt)))
        if stop == name:
            raise _Stop()

    nc = bass.Bass("TRN2", target_bir_lowering=False)
    L = n_layers_total

    def din(name, shape):
        return nc.dram_tensor(name, list(shape), F32, kind="ExternalInput").ap()

    x_in = din("x", [S_LEN, D])
    w_in = din("w_in", [L, D, IN_COLS])
    w_upa = din("w_up_a", [L, 512, D])
    w_upb = din("w_up_b", [L, 512, D])
    w_upc = din("w_up_c", [L, 512, D])
    w_out = din("w_out", [L, D, D])
    w_fg = din("ffn_w_gate", [L, D, DFF])
    w_fu = din("ffn_w_up", [L, D, DFF])
    w_fd = din("ffn_w_down", [L, DFF, D])
    pool_w = din("pool_w", [L, 4, 128, 128])
    wg2a = din("wg2a", [L, 17, 256])
    vecs = din("vecs", [L, 128, 24])
    rel_bias = din("rel_bias", [32, 4])
    c_ident = din("c_ident", [128, 128])
    c_tri = din("c_tri", [128, 256])
    c_bk = din("c_bk", [128, 384])
    c_negq = din("c_negq", [128, 8 * 32])
    c_pinv = din("c_pinv", [128, 64])
    y_out = nc.dram_tensor("y", [S_LEN, D], F32, kind="ExternalOutput").ap()
    x_mid = nc.dram_tensor("x_mid", [S_LEN, D], F32, kind="ExternalOutput").ap()
    dbg_out = None
    if debug:
        dbg_out = nc.dram_tensor("dbg", [128, 3, 4, S_LEN], F32, kind="ExternalOutput").ap()

    S = Sched(nc)
    ps_banks = [S.ps(f"psb{i}") for i in range(8)]
    ps_rr = {}

    def ps(lo=0, hi=8):
        n = hi - lo
        c = ps_rr.get((lo, hi), 0)
        ps_rr[(lo, hi)] = c + 1
        i = lo + c % n
        return ps_banks[i], f"ps{i}"

    identb = S.sb("identb", [128, 128], BF16)
    onesb = S.sb("onesb", [128, 128], BF16)
    tri = S.sb("tri", [128, 256], F32)
    trib = S.sb("trib", [128, 128], BF16)
    Tb = S.sb("Tb", [128, 4, 384], F32)
    relbc = S.sb("relbc", [128, 128], F32)
    negq = S.sb("negq", [128, 8, 32], F32)
    pinv = S.sb("pinv", [128, 4, 16], F32)
    vec = S.sb("vec", [128, 24], F32)
    epsc = S.sb("epsc", [128, 8], F32)
    pwb = S.sb("pwb", [128, 4, 128], BF16)
    ss = S.sb("ss", [128, 16], F32)
    rs = S.sb("rs", [128, 16], F32)
    rt = S.sb("rt", [128, 16], F32)
    sqj = [None]
    wg2 = S.sb("wg2", [17, 256], F32)
    NWB = 4
    wbs = [S.sb(f"wb{i}", [128, 4096], BF16) for i in range(NWB)]
    wb_rr = [0]
    REG0 = S.sb_off

    def R(name, shape, dtype, off):
        return S.sb(name + f"_{S.cnt['pe']}_{S.cnt['dve']}_{off}", shape, dtype, at=REG0 + off)

    KB = 1024
    hT = R("hT", [128, 8, S_LEN], BF16, 0)
    yT = [R(f"yT{i}", [128, 4, S_LEN], BF16, 32 * KB + i * 16 * KB) for i in range(3)]
    PH = 80 * KB
    assert REG0 + PH + 84 * KB <= S.sb_lim, (REG0, S.sb_lim)
    bk = R("bk", [128, 384], F32, PH + 12 * KB)
    ctmp = R("ctmp", [128, 384], F32, PH + 14 * KB)
    ctmp2 = R("ctmp2", [128, 384], F32, PH + 16 * KB)

    S.dma('sp', tri[:], c_tri[:, :], writes=['tri'])
    S.dma('sp', ctmp[:, 0:128], c_ident[:, :], writes=['ctmp'])
    S.dma('sp', bk[:], c_bk[:, :], writes=['bk'])
    S.dma('sp', relbc[:], rel_bias.rearrange("b h -> (b h)").partition_broadcast(128), writes=['relbc'])
    S.dma('sp', negq[:].rearrange("p a b -> p (a b)"), c_negq[:, :], writes=['negq'])
    S.dma('sp', pinv[:].rearrange("p a b -> p (a b)"), c_pinv[:, :], writes=['pinv'])
    S.op('dve', lambda e: e.tensor_copy(out=identb[:], in_=ctmp[:, 0:128]), reads=['ctmp'], writes=['identb'])
    S.op('dve', lambda e: e.tensor_copy(out=trib[:], in_=tri[:, 0:128]), reads=['tri'], writes=['trib'])
    S.op('dve', lambda e: e.memset(onesb[:], 1.0), writes=['onesb'])
    S.op('dve', lambda e: e.memset(epsc[:], EPS), writes=['epsc'])
    def build_tb():
        for h in range(4):
            S.op('dve', lambda e: e.tensor_scalar(out=Tb[:, h, :], in0=bk[:], scalar1=0.0, scalar2=NEG,
                                                  op0=ALU.is_lt, op1=ALU.mult), reads=['bk'], writes=[('Tb', h)])
        for b_ in range(32):
            S.op('dve', lambda e: e.tensor_scalar(out=ctmp2[:], in0=bk[:], scalar1=float(b_), scalar2=None,
                                                  op0=ALU.is_equal), reads=['bk'], writes=['ctmp2'])
            for h in range(4):
                S.op('dve', lambda e: e.scalar_tensor_tensor(
                    out=Tb[:, h, :], in0=ctmp2[:], scalar=relbc[:, b_ * 4 + h:b_ * 4 + h + 1], in1=Tb[:, h, :],
                    op0=ALU.mult, op1=ALU.add), reads=['ctmp2', 'relbc', ('Tb', h)], writes=[('Tb', h)])

    def load_wm(pieces, cast_eng=None):
        i = wb_rr[0] % NWB
        wb_rr[0] += 1
        wb = wbs[i]
        off = 0
        views = []
        for src_ap, K, ncols in pieces:
            n = K * ncols
            v = wb[:, off:off + n].rearrange("p (k c) -> p k c", k=K)
            S.dma('pool', v, src_ap.rearrange("(k p) c -> p k c", p=128), writes=[f'wb{i}'])
            views.append(v)
            off += n
        assert off <= 4096
        return views, f'wb{i}'

    def load_w(src_ap, K, ncols, cast_eng='pool'):
        views, key = load_wm([(src_ap, K, ncols)], cast_eng)
        return views[0], key

    def proj_fm(wb, wkey, K, ncols, srcT, srckey, evac, ts_list=(0, 1, 2, 3), pslo=0, pshi=8, defer=0):
        pend = []
        for m in range((ncols + 127) // 128):
            mc = min(128, ncols - m * 128)
            for ts in ts_list:
                pt, pk = ps(pslo, pshi)
                for k in range(K):
                    S.op('pe', lambda e: e.matmul(
                        pt[0:mc, 0:512], lhsT=wb[:, k, m * 128:m * 128 + mc], rhs=srcT[:, k, ts * 512:(ts + 1) * 512],
                        start=(k == 0), stop=(k == K - 1)), reads=[wkey, srckey], writes=[pk])
                pend.append((m, mc, ts, pt, pk))
                if len(pend) > defer:
                    evac(*pend.pop(0))
        while pend:
            evac(*pend.pop(0))

    def proj_tm(wb, wkey, K, ncols, srcT, srckey, evac, c0=0):
        for t in range(NT):
            pt, pk = ps()
            for k in range(K):
                S.op('pe', lambda e, pt=pt, k=k, t=t: e.matmul(
                    pt[:, 0:ncols], lhsT=srcT[:, k, t * 128:(t + 1) * 128], rhs=wb[:, k, c0:c0 + ncols],
                    start=(k == 0), stop=(k == K - 1)), reads=[wkey, srckey], writes=[pk])
            evac(t, pt, pk)

    def rstd_from_sum(ssum_ap, out_ap, tmp_ap, n, key_in, key_out, key_tmp):
        S.op('act', lambda e: e.activation(out=tmp_ap, in_=ssum_ap, func=AF.Ln, bias=epsc[:, 0:1], scale=1.0 / n),
             reads=[key_in, 'epsc'], writes=[key_tmp])
        S.op('act', lambda e: e.activation(out=out_ap, in_=tmp_ap, func=AF.Exp, scale=-0.5), reads=[key_tmp], writes=[key_out])

    def sumsq_tile(t, xt, xk, junk_ap):
        S.op('act', lambda e: e.activation(out=junk_ap, in_=xt, func=AF.Square, accum_out=ss[:, t:t + 1]),
             reads=[xk], writes=['junk', ('ss', t)])

    def norm_to_hT(l, get_tile, col0, NB=None, stats_ready=False):
        NB = PH if NB is None else NB
        junk = R("junk", [128, D], F32, NB + 1 * KB)
        xn = [R(f"xn{i}", [128, D], BF16, NB + 5 * KB + i * 2 * KB) for i in range(3)]
        tiles = []
        for t in range(NT):
            xt, xk = get_tile(t)
            tiles.append((xt, xk))
            if not stats_ready:
                sumsq_tile(t, xt, xk, junk[:])
        rstd_from_sum(ss[:], rs[:], rt[:], D, 'ss', 'rs', 'rt')
        for t in range(NT):
            xt, xk = tiles[t]
            xb = xn[t % 3]
            S.op('act', lambda e: e.mul(out=xb[:], in_=xt, mul=rs[:, t:t + 1]), reads=[xk, 'rs'], writes=[f'xn{t % 3}'])
            for half in range(2):
                pt, pk = ps()
                for kk in range(4):
                    k = half * 4 + kk
                    S.op('pe', lambda e: e.matmul(pt[:, kk * 128:(kk + 1) * 128], lhsT=xb[:, k * 128:(k + 1) * 128],
                                                  rhs=identb[:], start=True, stop=True),
                         reads=[f'xn{t % 3}', 'identb'], writes=[pk])
                for kk in range(4):
                    k = half * 4 + kk
                    S.op('dve', lambda e: e.tensor_scalar(
                        out=hT[:, k, t * 128:(t + 1) * 128], in0=pt[:, kk * 128:(kk + 1) * 128],
                        scalar1=vec[:, col0 + k:col0 + k + 1], scalar2=None, op0=ALU.mult),
                        reads=[pk, 'vec'], writes=[('hT', t, k)])

    xs_of = [None]

    def emit_layer(l, xin_ap, xout_ap, first):
        if not first:
            S.barrier()
        S.dma('sp', vec[:], vecs[l, :, :], writes=['vec'])
        S.dma('sp', wg2[:], wg2a[l, :, :], writes=['wg2'])
        WIN = w_in[l]

        if first:
            xall = R("xall", [128, NT, D], F32, PH + 18 * KB)

            def get_tile_dram(t):
                S.dma('sp', xall[:, t, :], xin_ap[t * 128:(t + 1) * 128, :], writes=[('xall', t)])
                return xall[:, t, :], ('xall', t)
            norm_to_hT(l, get_tile_dram, 0)
        else:
            xs_prev = xs_of[0]
            norm_to_hT(l, lambda t: (xs_prev[:, t, :], 'xs'), 0, PH + 48 * KB, stats_ready=True)
        if first:
            build_tb()
        S.barrier()
        chk('A')

        G0 = PH
        qTt = R("qTt", [128, 2, S_LEN], BF16, G0)
        kTz = R("kTz", [128, 2, 2, S_LEN], BF16, G0 + 8 * KB)
        ktl = R("ktl", [128, NT, 256], BF16, G0 + 24 * KB)
        E0 = G0 + 32 * KB
        vv = R("vv", [128, NT, 512], BF16, E0)
        expb = R("expb", [128, 2, S_LEN], F32, E0)
        expnb = R("expnb", [128, 2, S_LEN], F32, E0 + 16 * KB)
        etl = R("etl", [128, NT, 256], F32, E0 + 32 * KB)
        dec = R("dec", [128, 2, NT], F32, E0 + 48 * KB)
        Sst = R("Sst", [128, 2, 128], F32, E0 + 48 * KB + 256)
        Sbz = R("Sbz", [128, 2, 2, 128], BF16, E0 + 49 * KB + 256)
        lrT = R("lrT", [128, S_LEN], F32, G0)
        ez = [R(f"ez{i}", [128, 256], F32, G0 + 8 * KB + i * KB) for i in range(2)]
        lsp = [R(f"lsp{i}", [128, 256], F32, G0 + 10 * KB + i * KB) for i in range(2)]

        S.op('dve', lambda e: e.memset(lrT[0:32, :], 1.0), writes=['lrT'])
        wb, wk = load_w(WIN[:, C_LR:C_LR + 16], 8, 16)

        def ev_lr(m, mc, ts, pt, pk):
            S.op('act', lambda e: e.activation(out=lrT[0:16, ts * 512:(ts + 1) * 512], in_=pt[0:16, 0:512],
                                               func=AF.Copy), reads=[pk, 'lrT'], writes=[('lrTs', ts)])
        proj_fm(wb, wk, 8, 16, hT, 'hT', ev_lr)
        for t in range(NT):
            b = t % 2
            pz, pzk = ps()
            S.op('pe', lambda e, pz=pz, t=t: e.matmul(pz[:, 0:256], lhsT=lrT[0:17, t * 128:(t + 1) * 128],
                                                      rhs=wg2[0:17, :], start=True, stop=True),
                 reads=[('lrTs', t // 4), 'lrT', 'wg2'], writes=[pzk])
            S.op('act', lambda e, pz=pz, b=b: e.activation(out=ez[b][:], in_=pz[:, 0:256], func=AF.Exp, scale=-1.0),
                 reads=[pzk], writes=[f'ez{b}'])
            S.op('act', lambda e, b=b: e.activation(out=lsp[b][:], in_=ez[b][:], func=AF.Ln, bias=1.0),
                 reads=[f'ez{b}'], writes=[f'lsp{b}'])
            for c in range(2):
                pb, pbk = ps()
                S.op('pe', lambda e, pb=pb, b=b, c=c: e.matmul(pb[:, 0:128], lhsT=lsp[b][:, c * 128:(c + 1) * 128],
                                                               rhs=tri[:, 0:128], start=True, stop=True),
                     reads=[f'lsp{b}', 'tri'], writes=[pbk])
                S.op('act', lambda e, pb=pb, c=c, t=t: e.activation(out=expb[:, c, t * 128:(t + 1) * 128],
                                                                    in_=pb[:, 0:128], func=AF.Exp, scale=-1.0 / 16),
                     reads=[pbk], writes=[('expb', c, t)])
                S.op('act', lambda e, pb=pb, c=c, t=t: e.activation(out=expnb[:, c, t * 128:(t + 1) * 128],
                                                                    in_=pb[:, 0:128], func=AF.Exp, scale=1.0 / 16),
                     reads=[pbk], writes=[('expnb', c, t)])
                S.op('dve', lambda e, c=c, t=t: e.tensor_copy(out=dec[:, c, t:t + 1],
                                                              in_=expb[:, c, t * 128 + 127:t * 128 + 128]),
                     reads=[('expb', c, t)], writes=[('dec', c, t)])
            ptl, ptk = ps()
            S.op('pe', lambda e, ptl=ptl, b=b: e.matmul(ptl[:, 0:256], lhsT=tri[:, 128:256], rhs=lsp[b][:],
                                                        start=True, stop=True),
                 reads=[f'lsp{b}', 'tri'], writes=[ptk])
            S.op('act', lambda e, ptl=ptl, t=t: e.activation(out=etl[:, t, :], in_=ptl[:, 0:256], func=AF.Exp,
                                                             scale=-1.0 / 16), reads=[ptk], writes=[('etl', t)])
        S.barrier()

        chk('gla1')
        S.op('dve', lambda e: e.memset(kTz[:].rearrange("p a b c -> p (a b c)"), 0.0), writes=['kTz'])
        wb, wk = load_w(WIN[:, C_GQ:C_GQ + 256], 8, 256)

        def ev_q(m, mc, ts, pt, pk):
            S.op('dve', lambda e: e.scalar_tensor_tensor(
                out=qTt[:, m, ts * 512:(ts + 1) * 512], in0=pt[:, 0:512], scalar=0.125,
                in1=expb[:, m, ts * 512:(ts + 1) * 512], op0=ALU.mult, op1=ALU.mult),
                reads=[pk] + [('expb', m, ts * 4 + i) for i in range(4)], writes=[('qTt', m, ts)])
        proj_fm(wb, wk, 8, 256, hT, 'hT', ev_q)
        wb, wk = load_w(WIN[:, C_GK:C_GK + 256], 8, 256)

        def ev_k(m, mc, ts, pt, pk):
            for hh in range(2):
                r0, r1 = hh * 64, (hh + 1) * 64
                S.op('dve', lambda e, r0=r0, r1=r1, hh=hh: e.tensor_tensor(
                    out=kTz[r0:r1, m, hh, ts * 512:(ts + 1) * 512], in0=pt[r0:r1, 0:512],
                    in1=expnb[r0:r1, m, ts * 512:(ts + 1) * 512], op=ALU.mult),
                    reads=[pk, 'kTz'] + [('expnb', m, ts * 4 + i) for i in range(4)], writes=[('kTz', m, ts, hh)])
        proj_fm(wb, wk, 8, 256, hT, 'hT', ev_k)

        def ev_kt(t, pt, pk):
            S.op('dve', lambda e: e.tensor_tensor(out=ktl[:, t, :], in0=pt[:, 0:256], in1=etl[:, t, :], op=ALU.mult),
                 reads=[pk, ('etl', t)], writes=[('ktl', t)])
        proj_tm(wb, wk, 8, 256, hT, 'hT', ev_kt)
        S.barrier()
        wb, wk = load_w(WIN[:, C_GV:C_GV + 512], 8, 512)

        def ev_v(t, pt, pk):
            S.op('act', lambda e: e.activation(out=vv[:, t, :], in_=pt[:, 0:512], func=AF.Copy),
                 reads=[pk], writes=[('vv', t)])
        proj_tm(wb, wk, 8, 512, hT, 'hT', ev_v)
        S.barrier()

        chk('gla2')
        T0 = E0 + 16 * KB
        ATm = [R(f"ATm{i}", [128, 4, 128], BF16, T0 + i * KB) for i in range(2)]
        osq = [R(f"osq{i}", [128, 512], BF16, T0 + 2 * KB + i * KB) for i in range(3)]
        rsa = [R(f"rsa{i}", [128, 512], F32, T0 + 5 * KB + i * 2 * KB) for i in range(3)]
        rsb = rsa
        sil = [R(f"sil{i}", [128, 512], F32, PH + 80 * KB + i * 2 * KB) for i in range(2)]
        yaT = yT[0]
        S.op('dve', lambda e: e.memset(Sst[:].rearrange("p a b -> p (a b)"), 0.0), writes=['Sst'])
        S.op('dve', lambda e: e.memset(Sbz[:].rearrange("p a b c -> p (a b c)"), 0.0), writes=['Sbz'])
        po_of = {}

        def gla_pre(t):
            b = t % 2
            tok = slice(t * 128, (t + 1) * 128)
            pa, pak = ps(5, 7)
            for h in range(4):
                c, hh = h // 2, h % 2
                S.op('pe', lambda e: e.matmul(pa[:, h * 128:(h + 1) * 128], lhsT=kTz[:, c, hh, tok], rhs=qTt[:, c, tok],
                                              start=True, stop=True), reads=['kTz', 'qTt'], writes=[pak])
            for h in range(4):
                S.op('dve', lambda e: e.tensor_tensor(out=ATm[b][:, h, :], in0=pa[:, h * 128:(h + 1) * 128],
                                                      in1=tri[:, 0:128], op=ALU.mult),
                     reads=[pak, 'tri'], writes=[('ATm', b, h)])
            pd, pdk = ps(7, 8)
            for c in range(2):
                S.op('pe', lambda e: e.matmul(pd[:, c * 256:(c + 1) * 256], lhsT=ktl[:, t, c * 128:(c + 1) * 128],
                                              rhs=vv[:, t, c * 256:(c + 1) * 256], start=True, stop=True),
                     reads=['ktl', 'vv'], writes=[pdk])
            return pd, pdk

        def gla_o(t, pds):
            b = t % 2
            tok = slice(t * 128, (t + 1) * 128)
            po, pok = ps(0, 5)
            po_of[t] = (po, pok)
            for h in range(4):
                c, hh = h // 2, h % 2
                S.op('pe', lambda e: e.matmul(po[:, h * 128:(h + 1) * 128], lhsT=vv[:, t, h * 128:(h + 1) * 128],
                                              rhs=ATm[b][:, h, :], start=True, stop=False),
                     reads=['vv', ('ATm', b, h)], writes=[pok])
                S.op('pe', lambda e: e.matmul(po[:, h * 128:(h + 1) * 128], lhsT=Sbz[:, c, hh, :], rhs=qTt[:, c, tok],
                                              start=False, stop=True),
                     reads=[('Sbz', c, hh), 'Sbz', 'qTt'], writes=[pok])
            pd, pdk = pds
            for c in range(2):
                if t == NT - 1:
                    break
                for hh in range(2):
                    r0, r1 = hh * 64, (hh + 1) * 64
                    S.op('dve', lambda e: e.scalar_tensor_tensor(
                        out=Sst[r0:r1, c, :], in0=Sst[r0:r1, c, :], scalar=dec[r0:r1, c, t:t + 1],
                        in1=pd[r0:r1, c * 256 + hh * 128:c * 256 + (hh + 1) * 128], op0=ALU.mult, op1=ALU.add),
                        reads=[pdk, 'Sst', ('Sst', c, hh), 'dec'], writes=[('Sst', c, hh)])
                    S.op('act', lambda e: e.activation(out=Sbz[r0:r1, c, hh, :], in_=Sst[r0:r1, c, :], func=AF.Copy),
                         reads=[('Sst', c, hh), 'Sbz'], writes=[('Sbz', c, hh)])

        pss_of = {}

        def gla_n1(t):
            b = t % 3
            po, pok = po_of[t]
            S.op('act', lambda e: e.activation(out=osq[b][:], in_=po[:, 0:512], func=AF.Square), reads=[pok], writes=[f'osq{b}'])
            pss, pssk = ps(0, 5)
            pss_of[t] = (pss, pssk)
            S.op('pe', lambda e: e.matmul(pss[:, 0:512], lhsT=onesb[:], rhs=osq[b][:], start=True, stop=True),
                 reads=[f'osq{b}', 'onesb'], writes=[pssk])

        def gla_n2(t):
            b = t % 3
            tok = slice(t * 128, (t + 1) * 128)
            po, pok = po_of[t]
            pss, pssk = pss_of[t]
            rstd_from_sum(pss[:, 0:512], rsb[b][:], rsa[b][:], 128, pssk, f'rsa{b}', f'rsa{b}')
            S.op('dve', lambda e: e.scalar_tensor_tensor(
                out=yaT[:, :, tok], in0=po[:, 0:512].rearrange("p (h q) -> p h q", h=4), scalar=vec[:, 16:17],
                in1=rsb[b][:].rearrange("p (h q) -> p h q", h=4), op0=ALU.mult, op1=ALU.mult),
                reads=[pok, f'rsa{b}', 'vec'], writes=[('yaT', t)])

        for t in range(NT + 2):
            pds = gla_pre(t) if t < NT else None
            if 0 <= t - 1 < NT:
                gla_n1(t - 1)
            if 0 <= t - 2 < NT:
                gla_n2(t - 2)
            if t < NT:
                gla_o(t, pds)
        chk('gla3')
        puT = R("puT", [128, 4, S_LEN], F32, PH)
        pa_ = R("ppa", [128, S_LEN], F32, PH + 32 * KB)
        pb_ = R("ppb", [128, S_LEN], F32, PH + 40 * KB)
        ppb16 = R("ppb16", [128, 4, S_LEN], BF16, PH + 64 * KB)
        ybT = yT[1]
        S.dma('pool', pwb[:], pool_w[l].rearrange("g c e -> c g e"), writes=['pwb'])
        wb, wk = load_w(WIN[:, C_PU:C_PU + 512], 8, 512)

        def ev_pu(m, mc, ts, pt, pk):
            S.op('act', lambda e: e.activation(out=puT[:, m, ts * 512:(ts + 1) * 512], in_=pt[:, 0:512], func=AF.Copy),
                 reads=[pk], writes=[('puT', m)])
        proj_fm(wb, wk, 8, 512, hT, 'hT', ev_pu)
        pool_ops = []

        def pool_group(g):
            w = POOL_W[g]
            st8 = {'src': puT[:, g, :], 'sk': ('puT', g), 'bi': 0}
            bufs = [(pa_, 'ppa'), (pb_, 'ppb')]

            def shift(sh):
                def f():
                    dst, dk = bufs[st8['bi']]
                    st8['bi'] ^= 1
                    src, sk = st8['src'], st8['sk']
                    S.op('dve', lambda e: e.tensor_copy(out=dst[:, 0:sh], in_=src[:, 0:sh]), reads=[sk], writes=[dk])
                    S.op('dve', lambda e: e.tensor_tensor(out=dst[:, sh:S_LEN], in0=src[:, sh:S_LEN], in1=src[:, 0:S_LEN - sh],
                                                          op=ALU.add), reads=[sk, dk], writes=[dk])
                    st8['src'], st8['sk'] = dst[:], dk
                return f
            sh = 1
            while sh < w:
                pool_ops.append(shift(sh))
                sh *= 2

            def scale():
                dst, dk = bufs[st8['bi']]
                src, sk = st8['src'], st8['sk']
                S.op('dve', lambda e: e.scalar_tensor_tensor(
                    out=dst[:, 16:S_LEN], in0=src[:, 16:S_LEN], scalar=1.0 / w, in1=puT[:, g, 16:S_LEN],
                    op0=ALU.mult, op1=ALU.subtract), reads=[sk, ('puT', g), dk], writes=[dk])
                S.op('dve', lambda e: e.tensor_tensor(out=dst[:, 0:16], in0=src[:, 0:16], in1=pinv[:, g, :], op=ALU.mult),
                     reads=[sk, 'pinv', dk], writes=[dk])
                S.op('dve', lambda e: e.tensor_tensor(out=dst[:, 0:16], in0=dst[:, 0:16], in1=puT[:, g, 0:16], op=ALU.subtract),
                     reads=[dk, ('puT', g)], writes=[dk])
                S.op('act', lambda e: e.activation(out=ppb16[:, g, :], in_=dst[:], func=AF.Copy),
                     reads=[dk], writes=[('ppb16', g)])
            pool_ops.append(scale)

            def mm(ts):
                def f():
                    pt, pk = ps()
                    S.op('pe', lambda e: e.matmul(pt[:, 0:512], lhsT=pwb[:, g, :], rhs=ppb16[:, g, ts * 512:(ts + 1) * 512],
                                                  start=True, stop=True), reads=['pwb', ('ppb16', g)], writes=[pk])
                    S.op('dve', lambda e: e.tensor_scalar(
                        out=ybT[:, g, ts * 512:(ts + 1) * 512], in0=pt[:, 0:512], scalar1=vec[:, 17 + g:18 + g],
                        scalar2=None, op0=ALU.mult), reads=[pk, 'vec'], writes=[('ybT', g, ts)])
                return f
            for ts in range(4):
                pool_ops.append(mm(ts))

        for g in range(4):
            pool_group(g)
        chk('gla4')
        wb, wk = load_w(WIN[:, C_GR:C_GR + 512], 8, 512)

        def ev_r(m, mc, ts, pt, pk):
            b = (m * 4 + ts) % 2
            S.op('act', lambda e: e.activation(out=sil[b][:], in_=pt[:, 0:512], func=AF.Silu),
                 reads=[pk], writes=[f'sil{b}'])
            S.op('dve', lambda e: e.tensor_tensor(out=yaT[:, m, ts * 512:(ts + 1) * 512],
                                                  in0=yaT[:, m, ts * 512:(ts + 1) * 512], in1=sil[b][:], op=ALU.mult),
                 reads=[f'sil{b}'] + [('yaT', ts * 4 + i) for i in range(4)], writes=[('yaTf', m, ts)])
            for _ in range(2):
                if pool_ops:
                    pool_ops.pop(0)()
        proj_fm(wb, wk, 8, 512, hT, 'hT', ev_r)
        while pool_ops:
            pool_ops.pop(0)()
        S.barrier()
        chk('pool')
        emit_moba(l, WIN)
        S.barrier()
        chk('moba')
        emit_merge_ffn(l, WIN, xin_ap, xout_ap, xout_ap is y_out)

    def emit_moba(l, WIN):
        M0 = PH
        qhT = R("qhT", [128, 4, S_LEN], BF16, M0)
        khT = R("khT", [128, 4, S_LEN], BF16, M0 + 16 * KB)
        v1 = R("v1", [128, NT, 4, 130], BF16, M0 + 32 * KB)
        X0 = M0 + 49 * KB
        NPB = 4
        sq = [R(f"msq{i}", [128, 512], BF16, X0 + i * KB) for i in range(NPB)]
        mra = [R(f"mra{i}", [128, 512], F32, X0 + 4 * KB + i * 2 * KB) for i in range(NPB)]
        mrb = [R(f"mrb{i}", [128, 512], F32, X0 + 12 * KB + i * 2 * KB) for i in range(NPB)]
        kmf = R("kmf", [128, 4, 8], F32, X0)
        kmb = R("kmb", [128, 4, 8], BF16, X0 + 128)
        msk = R("msk", [128, 32], F32, X0 + 256)
        top8 = R("top8", [128, 4, 8], F32, X0 + 384)
        sel = R("sel", [128, NT, 32], F32, X0 + 1 * KB)
        NPT = 6
        PTb = [R(f"PTb{i}", [128, 256], BF16, X0 + 3 * KB + i * 512) for i in range(NPT)]
        ltm = [R(f"ltm{i}", [128, 256], F32, X0 + 6 * KB + i * KB) for i in range(4)]
        acc = [[R(f"acc{a_}{b_}", [128, 4, 132], F32, X0 + 10 * KB + (a_ * 2 + b_) * 2112) for b_ in range(2)] for a_ in range(2)]
        rec = R("rec", [128, 2, 4], F32, X0 + 19 * KB)
        ycn = [R(f"ycn{i}", [128, 512], BF16, X0 + 20 * KB + i * KB) for i in range(2)]
        ycT = yT[2]
        scale = 128.0 ** -0.5

        wq, wqk = load_w(WIN[:, C_MQ:C_MQ + 512], 8, 512)
        wk_, wkk = load_w(WIN[:, C_MK:C_MK + 512], 8, 512)
        groups = []
        for (wb, wkey, dstT, wcol, name) in ((wq, wqk, qhT, 21, 'qhT'), (wk_, wkk, khT, 22, 'khT')):
            for m in range(4):
                for ts in range(4):
                    groups.append(dict(wb=wb, wkey=wkey, dstT=dstT, wcol=wcol, name=name, m=m, ts=ts))
        NG = len(groups)

        def st_mm(i):
            g = groups[i]
            pt, pk = ps(0, 4)
            g['pt'], g['pk'] = pt, pk
            for k in range(8):
                S.op('pe', lambda e: e.matmul(pt[:, 0:512], lhsT=g['wb'][:, k, g['m'] * 128:(g['m'] + 1) * 128],
                                              rhs=hT[:, k, g['ts'] * 512:(g['ts'] + 1) * 512], start=(k == 0), stop=(k == 7)),
                     reads=[g['wkey'], 'hT'], writes=[pk])

        def st_sq(i):
            g = groups[i]
            b_ = i % NPB
            S.op('act', lambda e: e.activation(out=sq[b_][:], in_=g['pt'][:, 0:512], func=AF.Square),
                 reads=[g['pk']], writes=[f'msq{b_}'])
            pss, pssk = ps(4, 8)
            g['pss'], g['pssk'] = pss, pssk
            S.op('pe', lambda e: e.matmul(pss[:, 0:512], lhsT=onesb[:], rhs=sq[b_][:], start=True, stop=True),
                 reads=[f'msq{b_}', 'onesb'], writes=[pssk])

        def st_rt(i):
            g = groups[i]
            b_ = i % NPB
            S.op('act', lambda e: e.activation(out=mra[b_][:], in_=g['pss'][:, 0:512], func=AF.Ln, bias=epsc[:, 0:1],
                                               scale=1.0 / 128), reads=[g['pssk'], 'epsc'], writes=[f'mra{b_}'])
            S.op('act', lambda e: e.activation(out=mrb[b_][:], in_=mra[b_][:], func=AF.Exp, scale=-0.5),
                 reads=[f'mra{b_}'], writes=[f'mrb{b_}'])

        def st_fin(i):
            g = groups[i]
            b_ = i % NPB
            S.op('dve', lambda e: e.scalar_tensor_tensor(
                out=g['dstT'][:, g['m'], g['ts'] * 512:(g['ts'] + 1) * 512], in0=g['pt'][:, 0:512],
                scalar=vec[:, g['wcol']:g['wcol'] + 1], in1=mrb[b_][:], op0=ALU.mult, op1=ALU.mult),
                reads=[g['pk'], f'mrb{b_}', 'vec'], writes=[(g['name'], g['m'], g['ts'])])

        for i in range(NG + 3):
            if i < NG:
                st_mm(i)
            if 0 <= i - 1 < NG:
                st_sq(i - 1)
            if 0 <= i - 2 < NG:
                st_rt(i - 2)
            if 0 <= i - 3 < NG:
                st_fin(i - 3)
        S.op('dve', lambda e: e.memset(v1[:].rearrange("p a b c -> p (a b c)"), 1.0), writes=['v1'])
        wb, wk = load_w(WIN[:, C_MV:C_MV + 512], 8, 512)

        def ev_mv(t, pt, pk):
            S.op('act', lambda e: e.activation(out=v1[:, t, :, 0:128],
                                               in_=pt[:, 0:512].rearrange("p (h d) -> p h d", h=4), func=AF.Copy),
                 reads=[pk, 'v1'], writes=[('v1', t)])
        proj_tm(wb, wk, 8, 512, hT, 'hT', ev_mv)
        S.barrier()
        chk('moba_proj')
        for h in range(4):
            S.op('dve', lambda e, h=h: e.tensor_reduce(out=kmf[:, h, :],
                                                       in_=khT[:, h, :].rearrange("p (n j) -> p n j", n=8),
                                                       axis=AX.X, op=ALU.add), reads=['khT'], writes=[('kmf', h)])
        S.op('dve', lambda e: e.tensor_scalar(out=kmb[:].rearrange("p a b -> p (a b)"),
                                              in0=kmf[:].rearrange("p a b -> p (a b)"), scalar1=1.0 / 256,
                                              scalar2=None, op0=ALU.mult),
             reads=[('kmf', h) for h in range(4)], writes=['kmb'])
        for i in range(2, NT):
            qb = i // 2
            pr, prk = ps(4, 8)
            for h in range(4):
                S.op('pe', lambda e, pr=pr, h=h, i=i: e.matmul(pr[:, h * 8:(h + 1) * 8], lhsT=qhT[:, h, i * 128:(i + 1) * 128],
                                                              rhs=kmb[:, h, :], start=True, stop=True),
                     reads=['kmb', 'qhT'], writes=[prk])
            S.op('dve', lambda e, pr=pr, qb=qb: e.tensor_tensor(out=msk[:], in0=pr[:, 0:32], in1=negq[:, qb, :], op=ALU.add),
                 reads=[prk, 'negq'], writes=['msk'])
            for h in range(4):
                S.op('dve', lambda e, h=h: e.max(out=top8[:, h, :], in_=msk[:, h * 8:(h + 1) * 8]),
                     reads=['msk'], writes=[('top8', h)])
            for h in range(4):
                S.op('dve', lambda e, h=h, i=i: e.tensor_scalar(out=sel[:, i, h * 8:(h + 1) * 8], in0=msk[:, h * 8:(h + 1) * 8],
                                                                scalar1=top8[:, h, 2:3], scalar2=None, op0=ALU.is_ge),
                     reads=['msk', ('top8', h)], writes=[('sel', i)])
        chk('moba_route')
        pt_rr = [0]

        def stage1(tk):
            h, jj, ncol, q0, bias_ap = tk['h'], tk['jj'], tk['ncol'], tk['q0'], tk['bias']
            st, stk = ps(4, 8)
            S.op('pe', lambda e: e.matmul(st[:, 0:ncol], lhsT=khT[:, h, jj * 128:(jj + 1) * 128],
                                          rhs=qhT[:, h, q0:q0 + ncol], start=True, stop=True),
                 reads=['qhT', 'khT'], writes=[stk])
            pi = pt_rr[0] % NPT
            pt_rr[0] += 1
            P = PTb[pi]
            if bias_ap is None:
                S.op('act', lambda e: e.activation(out=P[:, 0:ncol], in_=st[:, 0:ncol], func=AF.Exp,
                                                   bias=relbc[:, 124 + h:125 + h], scale=scale),
                     reads=[stk, 'relbc'], writes=[f'PTb{pi}'])
            else:
                li = pt_rr[0] % 4
                S.op('dve', lambda e: e.scalar_tensor_tensor(out=ltm[li][:, 0:ncol], in0=st[:, 0:ncol], scalar=scale,
                                                             in1=bias_ap, op0=ALU.mult, op1=ALU.add),
                     reads=[stk, ('Tb', h)], writes=[f'ltm{li}'])
                S.op('act', lambda e: e.activation(out=P[:, 0:ncol], in_=ltm[li][:, 0:ncol], func=AF.Exp),
                     reads=[f'ltm{li}'], writes=[f'PTb{pi}'])
            tk['P'], tk['Pk'] = P, f'PTb{pi}'

        def stage2(tk):
            h, jj, P, Pk = tk['h'], tk['jj'], tk['P'], tk['Pk']
            for grp, cb, st_, sp_ in tk['pv']:
                if grp['bank'] is None:
                    grp['bank'] = ps(0, 4)
                pb, pbk = grp['bank']
                S.op('pe', lambda e: e.matmul(pb[:, 0:129], lhsT=P[:, cb * 128:(cb + 1) * 128], rhs=v1[:, jj, h, 0:129],
                                              start=st_, stop=sp_), reads=[Pk, 'v1'], writes=[pbk])
            for po_ in tk['post']:
                if po_[0] == 'copy':
                    _, grp, qi, par = po_
                    pb, pbk = grp['bank']
                    S.op('dve', lambda e: e.tensor_copy(out=acc[qi][par][:, h, 0:129], in_=pb[:, 0:129]),
                         reads=[pbk], writes=[('acc', qi, par, h)])
                elif po_[0] == 'acc':
                    _, grp, qi, par, n, ii = po_
                    pb, pbk = grp['bank']
                    S.op('dve', lambda e: e.scalar_tensor_tensor(
                        out=acc[qi][par][:, h, 0:129], in0=pb[:, 0:129], scalar=sel[:, ii, h * 8 + n:h * 8 + n + 1],
                        in1=acc[qi][par][:, h, 0:129], op0=ALU.mult, op1=ALU.add),
                        reads=[pbk, ('sel', ii), ('acc', qi, par, h)], writes=[('acc', qi, par, h)])
                else:
                    finalize(po_[1])

        def finalize(QB):
            par = QB % 2
            for qi in range(2):
                ii = 2 * QB + qi
                A = acc[qi][par]
                S.op('dve', lambda e: e.reciprocal(out=rec[:, qi, :], in_=A[:, :, 128]),
                     reads=[('acc', qi, par, h) for h in range(4)], writes=[('rec', qi)])
                for h in range(4):
                    S.op('dve', lambda e: e.tensor_scalar(
                        out=ycn[qi][:, h * 128:(h + 1) * 128], in0=A[:, h, 0:128], scalar1=rec[:, qi, h:h + 1],
                        scalar2=None, op0=ALU.mult), reads=[('rec', qi), ('acc', qi, par, h)], writes=[('ycn', qi, h)])
                ptp, ptk = ps(4, 8)
                for h in range(4):
                    S.op('pe', lambda e: e.matmul(ptp[:, h * 128:(h + 1) * 128], lhsT=ycn[qi][:, h * 128:(h + 1) * 128],
                                                  rhs=identb[:], start=True, stop=True),
                         reads=[('ycn', qi, h), 'identb'], writes=[ptk])
                S.op('act', lambda e: e.activation(out=ycT[:, :, ii * 128:(ii + 1) * 128],
                                                   in_=ptp[:, 0:512].rearrange("p (h q) -> p h q", h=4), func=AF.Copy),
                     reads=[ptk], writes=[('ycT', ii)])

        tasks = []
        for QB in range(8):
            par = QB % 2
            i0, i1 = 2 * QB, 2 * QB + 1
            for h in range(4):
                g0, g1 = {'bank': None}, {'bank': None}
                tasks.append(dict(h=h, jj=i0, ncol=256, q0=QB * 256, bias=Tb[:, h, 0:256],
                                  pv=[(g0, 0, True, True), (g1, 1, True, False)], post=[('copy', g0, 0, par)]))
                tasks.append(dict(h=h, jj=i1, ncol=128, q0=i1 * 128, bias=Tb[:, h, 0:128],
                                  pv=[(g1, 0, False, True)], post=[('copy', g1, 1, par)]))
                for n in range(QB):
                    a0, a1 = {'bank': None}, {'bank': None}
                    for jj in (2 * n, 2 * n + 1):
                        first, last = (jj == 2 * n), (jj == 2 * n + 1)
                        tasks.append(dict(h=h, jj=jj, ncol=256, q0=QB * 256,
                                          bias=(Tb[:, h, 128:384] if jj == 2 * QB - 1 else None),
                                          pv=[(a0, 0, first, last), (a1, 1, first, last)],
                                          post=([('acc', a0, 0, par, n, i0), ('acc', a1, 1, par, n, i1)] if last else [])))
            tasks[-1]['post'] = list(tasks[-1]['post']) + [('fin', QB)]
        DEPTH = 3
        for i in range(len(tasks) + DEPTH):
            if i < len(tasks):
                stage1(tasks[i])
            if i >= DEPTH:
                stage2(tasks[i - DEPTH])

    def emit_merge_ffn(l, WIN, xin_ap, xout_ap, is_final):
        mT = R("mT", [128, 8, S_LEN], BF16, PH + 16 * KB)
        sg = [R(f"sg{i}", [128, 512], F32, PH + 48 * KB + i * 2 * KB) for i in range(3)]
        pr_ = [R(f"pr{i}", [128, 512], F32, PH + 54 * KB + i * 2 * KB) for i in range(3)]
        ups = [w_upa[l], w_upb[l], w_upc[l]]
        gcols = [C_GA, C_GB, C_GC]

        def merge_step(m, ts, gv, gk, uv, uk):
            pgs, pus = [], []
            for x in range(3):
                pg, pgk = ps()
                for k in range(8):
                    S.op('pe', lambda e: e.matmul(pg[:, 0:512], lhsT=gv[x][:, k, :], rhs=hT[:, k, ts * 512:(ts + 1) * 512],
                                                  start=(k == 0), stop=(k == 7)), reads=[gk, 'hT'], writes=[pgk])
                pgs.append((pg, pgk))
            for x in range(3):
                pu_, puk = ps()
                for k in range(4):
                    S.op('pe', lambda e: e.matmul(pu_[:, 0:512], lhsT=uv[x][:, k, :], rhs=yT[x][:, k, ts * 512:(ts + 1) * 512],
                                                  start=(k == 0), stop=(k == 3)), reads=[uk, ('yaTf', 'ybT', 'ycT')[x]], writes=[puk])
                pus.append((pu_, puk))
            for x in range(3):
                S.op('act', lambda e: e.activation(out=sg[x][:], in_=pgs[x][0][:, 0:512], func=AF.Sigmoid),
                     reads=[pgs[x][1]], writes=[f'sg{x}'])
                S.op('dve', lambda e: e.tensor_tensor(out=pr_[x][:], in0=pus[x][0][:, 0:512], in1=sg[x][:], op=ALU.mult),
                     reads=[pus[x][1], f'sg{x}'], writes=[f'pr{x}'])
            S.op('dve', lambda e: e.tensor_tensor(out=pr_[0][:], in0=pr_[0][:], in1=pr_[1][:], op=ALU.add),
                 reads=['pr0', 'pr1'], writes=['pr0'])
            S.op('dve', lambda e: e.tensor_tensor(out=mT[:, m, ts * 512:(ts + 1) * 512], in0=pr_[0][:], in1=pr_[2][:],
                                                   op=ALU.add), reads=['pr0', 'pr2'], writes=[('mT', m, ts)])

        for m in range(8):
            gv, gk = load_wm([(WIN[:, gcols[x] + m * 128:gcols[x] + (m + 1) * 128], 8, 128) for x in range(3)])
            uv, uk = load_wm([(ups[x][:, m * 128:(m + 1) * 128], 4, 128) for x in range(3)])
            for ts in range(4):
                merge_step(m, ts, gv, gk, uv, uk)
        S.barrier()
        chk('D')
        xs = S.sb(f"xs_{l}", [128, NT, D], F32, at=REG0 + 32 * KB)
        xld = [R(f"xld{i}", [128, D], F32, PH + 48 * KB + i * 4 * KB) for i in range(2)]
        sqjunk = R("sqjunk", [128, D], F32, PH + 56 * KB)
        xs_of[0] = xs
        wo = [load_w(w_out[l][:, hf * 512:(hf + 1) * 512], 8, 512) for hf in range(2)]

        def out_tile(t):
            b = t % 2
            S.dma('sp', xld[b][:], xin_ap[t * 128:(t + 1) * 128, :], writes=[f'xld{b}'])
            for hf in range(2):
                pt, pk = ps()
                for k in range(8):
                    S.op('pe', lambda e: e.matmul(pt[:, 0:512], lhsT=mT[:, k, t * 128:(t + 1) * 128], rhs=wo[hf][0][:, k, :],
                                                  start=(k == 0), stop=(k == 7)), reads=[wo[hf][1], 'mT'], writes=[pk])
                S.op('dve', lambda e: e.tensor_tensor(out=xs[:, t, hf * 512:(hf + 1) * 512], in0=pt[:, 0:512],
                                                      in1=xld[b][:, hf * 512:(hf + 1) * 512], op=ALU.add),
                     reads=[pk, f'xld{b}'], writes=[('xs', t, hf)])
            sumsq_tile(t, xs[:, t, :], 'xs', sqjunk[:])
        for t in range(NT):
            out_tile(t)
        S.barrier()

        chk('E')
        def get_tile_sb(t):
            return xs[:, t, :], 'xs'
        norm_to_hT(l, get_tile_sb, 8, PH + 48 * KB, stats_ready=True)
        S.barrier()
        chk('F')
        aT = [R(f"aT{i}", [128, 4, S_LEN], BF16, PH + 16 * KB + i * 16 * KB) for i in range(2)]
        st_ = [R(f"fs{i}", [128, 512], F32, PH + 48 * KB + i * 2 * KB) for i in range(3)]
        gjunk = R("gjunk", [128, D], F32, PH + 56 * KB)
        groups = [(g * 4, min(4, KFF - g * 4)) for g in range((KFF + 3) // 4)]

        def ffn_gu(gi, m, ts, A, gv, gk, uv, uk):
            pg, pgk = ps()
            pu_, puk = ps()
            for k in range(8):
                S.op('pe', lambda e: e.matmul(pg[:, 0:512], lhsT=gv[:, k, m * 128:(m + 1) * 128], rhs=hT[:, k, ts * 512:(ts + 1) * 512],
                                              start=(k == 0), stop=(k == 7)), reads=[gk, 'hT'], writes=[pgk])
            for k in range(8):
                S.op('pe', lambda e: e.matmul(pu_[:, 0:512], lhsT=uv[:, k, m * 128:(m + 1) * 128], rhs=hT[:, k, ts * 512:(ts + 1) * 512],
                                              start=(k == 0), stop=(k == 7)), reads=[uk, 'hT'], writes=[puk])
            si = (m * 4 + ts) % 3
            S.op('act', lambda e: e.activation(out=st_[si][:], in_=pg[:, 0:512], func=AF.Silu), reads=[pgk], writes=[f'fs{si}'])
            S.op('dve', lambda e: e.tensor_tensor(out=A[:, m, ts * 512:(ts + 1) * 512], in0=pu_[:, 0:512], in1=st_[si][:], op=ALU.mult),
                 reads=[puk, f'fs{si}'], writes=[('aT', gi % 2, m, ts)])

        def ffn_down(gi, t, A, kn, dv, dk, last):
            for hf in range(2):
                pt, pk = ps()
                for k in range(kn):
                    S.op('pe', lambda e: e.matmul(pt[:, 0:512], lhsT=A[:, k, t * 128:(t + 1) * 128],
                                                  rhs=dv[:, k, hf * 512:(hf + 1) * 512], start=(k == 0), stop=(k == kn - 1)),
                         reads=[dk, ('aT', gi % 2, k, t // 4)], writes=[pk])
                S.op('dve', lambda e: e.tensor_tensor(out=xs[:, t, hf * 512:(hf + 1) * 512], in0=xs[:, t, hf * 512:(hf + 1) * 512],
                                                      in1=pt[:, 0:512], op=ALU.add),
                     reads=[pk, ('xs', t, hf)], writes=[('xs', t, hf)])
            if last:
                S.dma('sp', xout_ap[t * 128:(t + 1) * 128, :], xs[:, t, :], reads=[('xs', t, 0), ('xs', t, 1)], writes=['xout'])
                if not is_final:
                    sumsq_tile(t, xs[:, t, :], 'xs', gjunk[:])

        for gi, (k0, kn) in enumerate(groups):
            ncl = kn * 128
            A = aT[gi % 2]
            gv, gk = load_w(w_fg[l][:, k0 * 128:k0 * 128 + ncl], 8, ncl)
            uv, uk = load_w(w_fu[l][:, k0 * 128:k0 * 128 + ncl], 8, ncl)
            for m in range(kn):
                for ts in range(4):
                    ffn_gu(gi, m, ts, A, gv, gk, uv, uk)
            dv, dk = load_w(w_fd[l][k0 * 128:k0 * 128 + ncl, :], kn, 1024)
            for t in range(NT):
                ffn_down(gi, t, A, kn, dv, dk, gi == len(groups) - 1)

    n = len(layers)
    try:
        for idx, l in enumerate(layers):
            xin_ap = x_in if idx == 0 else x_mid
            xout_ap = y_out if idx == n - 1 else x_mid
            emit_layer(l, xin_ap, xout_ap, idx == 0)
    except _Stop:
        pass
    if 'xout' not in S.dsem:
        S.barrier()
        S.dma('sp', y_out[0:128, 0:256], tri[:, 0:256], writes=['xout'])
    if debug:
        S.barrier()
        dtmp = R("dtmp", [128, 4, S_LEN], F32, PH)
        srcs = [yT[0][:], yT[1][:], yT[2][:]] if stop != 'A' else [hT[:, 0:4, :], hT[:, 4:8, :], yT[2][:]]
        for i in range(3):
            S.op('dve', lambda e, i=i: e.tensor_copy(out=dtmp[:], in_=srcs[i]),
                 reads=['dbgx'], writes=['dtmp'])
            S.dma('sp', dbg_out[:, i, :, :], dtmp[:], reads=['dtmp'], writes=['dbgx'])
        S.finish(['xout', 'dbgx'])
    else:
        S.finish(['xout'])
    S.marks = marks
    return nc, S


def t5_bucket_np(dist):
    n = np.maximum(dist, 0)
    large = 16 + (np.log(np.maximum(n, 1).astype(np.float32) / np.float32(16)) / np.float32(np.log(128 / 16))
                  * np.float32(16)).astype(np.int32)
    large = np.minimum(large, 31)
    return np.where(n < 16, n, large)


def host_constants():
    c = {}
    c["c_ident"] = np.eye(128, dtype=np.float32)
    j = np.arange(128)[:, None]
    i = np.arange(128)[None, :]
    c["c_tri"] = np.concatenate([(j <= i), (j > i)], axis=1).astype(np.float32)
    k = np.arange(128)[:, None]
    m = np.arange(384)[None, :]
    dist = m - k
    c["c_bk"] = np.where(dist >= 0, t5_bucket_np(dist), -1).astype(np.float32)
    nq = np.zeros((8, 4, 8), np.float32)
    for qb in range(8):
        nq[qb, :, qb:] = -1e30
    c["c_negq"] = np.ascontiguousarray(np.broadcast_to(nq.reshape(1, 256), (128, 256))).astype(np.float32)
    pv = np.zeros((4, 16), np.float32)
    for g, w in enumerate(POOL_W):
        pv[g] = 1.0 / np.minimum(np.arange(16) + 1, w)
    c["c_pinv"] = np.ascontiguousarray(np.broadcast_to(pv.reshape(1, 64), (128, 64))).astype(np.float32)
    return c


def pack_small(inputs):
    L = inputs["norm1_w"].shape[0]
    vecs = np.zeros((L, 128, 24), np.float32)
    for l in range(L):
        vecs[l, :, 0:8] = np.asarray(inputs["norm1_w"][l]).reshape(8, 128).T
        vecs[l, :, 8:16] = np.asarray(inputs["norm2_w"][l]).reshape(8, 128).T
        vecs[l, :, 16] = np.asarray(inputs["gla_norm_w"][l])
        vecs[l, :, 17:21] = np.asarray(inputs["pool_scale"][l]).reshape(4, 128).T
        vecs[l, :, 21] = np.asarray(inputs["moba_qn_w"][l])
        vecs[l, :, 22] = np.asarray(inputs["moba_kn_w"][l])
    wg2a = np.concatenate([np.asarray(inputs["gla_wg2"]), np.asarray(inputs["gla_bg"])[:, None, :]], axis=1)
    return vecs, np.ascontiguousarray(wg2a.astype(np.float32))


_CACHE = {}


def make_in_maps(inputs, x_full):
    vecs, wg2a = pack_small(inputs)
    consts = host_constants()
    shared = {k: np.ascontiguousarray(np.asarray(inputs[k], dtype=np.float32)) for k in
              ("w_in", "w_up_a", "w_up_b", "w_up_c", "w_out", "ffn_w_gate", "ffn_w_up", "ffn_w_down", "pool_w",
               "rel_bias")}
    shared["vecs"] = vecs
    shared["wg2a"] = wg2a
    shared.update(consts)
    maps = []
    for c in range(8):
        m = dict(shared)
        m["x"] = np.ascontiguousarray(x_full[c])
        maps.append(m)
    return maps


FUSED = True


def kernel(**inputs):
    x = np.asarray(inputs["x"], dtype=np.float32)
    if FUSED:
        if "fused" not in _CACHE:
            _CACHE["fused"] = build_program([0, 1])[0]
        res = run_bass_kernel_spmd(_CACHE["fused"], make_in_maps(inputs, x), core_ids=list(range(8)))
        return np.stack([np.asarray(r["y"]) for r in res.results], axis=0).astype(np.float32)
    for l in range(2):
        if l not in _CACHE:
            _CACHE[l] = build_program([l])[0]
        res = run_bass_kernel_spmd(_CACHE[l], make_in_maps(inputs, x), core_ids=list(range(8)))
        x = np.stack([np.asarray(r["y"]) for r in res.results], axis=0).astype(np.float32)
    return x
```

```python
import numpy as np
import concourse.bass as bass
import concourse.mybir as mybir
from concourse.bass_utils import run_bass_kernel_spmd

F32 = mybir.dt.float32
BF16 = mybir.dt.bfloat16
AF = mybir.ActivationFunctionType
ALU = mybir.AluOpType
AX = mybir.AxisListType


class _Rec:
    def __getattr__(self, name):
        def f(*a, **kw):
            return (name, a, kw)
        return f


_REC = _Rec()


class Sched:
    ENGS = ('pe', 'act', 'dve', 'pool', 'sp')

    def __init__(self, nc):
        self.nc = nc
        self.prog = {e: [] for e in self.ENGS}
        self.cnt = {e: 0 for e in self.ENGS}
        self.esem = {e: nc.alloc_semaphore(f"es_{e}") for e in ('pe', 'act', 'dve', 'pool')}
        self.semeng = {id(s): e for e, s in self.esem.items()}
        self.waited = {e: {} for e in self.ENGS}
        self.lastw = {}
        self.readers = {}
        self.groupw = {}
        self.groupr = {}
        self.dsem = {}
        self.dcnt = {}
        self.sb_off = 16512
        self.sb_lim = 229344
        self.nps = 0
        self.nwaits = 0

    def sb(self, name, shape, dtype, at=None):
        esz = 4 if dtype == F32 else 2
        n = 1
        for s in shape[1:]:
            n *= s
        nbytes = (n * esz + 31) // 32 * 32
        if at is None:
            at = self.sb_off
            self.sb_off += nbytes
            assert self.sb_off <= self.sb_lim, (name, self.sb_off)
        return self.nc.alloc_sbuf_tensor_at(name, list(shape), dtype, offset=at)

    def ps(self, name, shape=(128, 512), dtype=F32):
        return self.nc.alloc_psum_tensor(name, list(shape), dtype)

    @staticmethod
    def _grp(key):
        return key[0] if isinstance(key, tuple) else key

    def _deps(self, eng, reads, writes):
        deps = {}

        def add(ev, kind):
            sem, val, e2 = ev
            if e2 == eng and (eng == 'pe' or kind == 'war'):
                return
            k = id(sem)
            if k not in deps or deps[k][1] < val:
                deps[k] = (sem, val)

        for key in reads:
            if key in self.lastw:
                add(self.lastw[key], 'raw')
            if not isinstance(key, tuple):
                for ev in self.groupw.get(key, {}).values():
                    add(ev, 'raw')
            else:
                g = key[0]
                if g in self.lastw:
                    add(self.lastw[g], 'raw')
        for key in writes:
            g = self._grp(key)
            if key in self.lastw:
                add(self.lastw[key], 'waw')
            for ev in self.readers.get(key, {}).values():
                add(ev, 'war')
            if isinstance(key, tuple):
                if g in self.lastw:
                    add(self.lastw[g], 'waw')
                for ev in self.readers.get(g, {}).values():
                    add(ev, 'war')
            else:
                for ev in self.groupw.get(g, {}).values():
                    add(ev, 'waw')
                for ev in self.groupr.get(g, {}).values():
                    add(ev, 'war')
        out = []
        w = self.waited[eng]
        for k, (sem, val) in deps.items():
            if w.get(k, 0) >= val:
                continue
            w[k] = val
            out.append((sem, val))
        return out

    def _commit(self, ev, reads, writes):
        k = id(ev[0])
        for key in reads:
            self.readers.setdefault(key, {})[k] = ev
            if isinstance(key, tuple):
                self.groupr.setdefault(key[0], {})[k] = ev
        for key in writes:
            self.lastw[key] = ev
            self.readers[key] = {}
            if isinstance(key, tuple):
                self.groupw.setdefault(key[0], {})[k] = ev
            else:
                self.groupw[key] = {}
                self.groupr[key] = {}

    def op(self, eng, fn, reads=(), writes=()):
        for sem, val in self._deps(eng, reads, writes):
            self.prog[eng].append(('w', sem, val))
            self.nwaits += 1
        self.cnt[eng] += 1
        sem = self.esem[eng]
        self.prog[eng].append(('o', fn(_REC), sem, 1))
        self._commit((sem, self.cnt[eng], eng), reads, writes)

    def dma(self, q, out, in_, reads=(), writes=()):
        assert len(writes) == 1
        dk = writes[0]
        for sem, val in self._deps(q, reads, writes):
            self.prog[q].append(('w', sem, val))
            self.nwaits += 1
        if dk not in self.dsem:
            self.dsem[dk] = self.nc.alloc_semaphore(f"ds{len(self.dsem)}")
            self.dcnt[dk] = 0
        self.dcnt[dk] += 16
        sem = self.dsem[dk]
        self.prog[q].append(('o', ('dma_start', (), dict(out=out, in_=in_)), sem, 16))
        self._commit((sem, self.dcnt[dk], None), reads, writes)

    def finish(self, outkeys):
        for dk in outkeys:
            sem, val = self.dsem[dk], self.dcnt[dk]
            self.prog['sp'].append(('w', sem, val))
        nc = self.nc
        prog = self.prog

        def replay(eng, items):
            for it in items:
                if it[0] == 'w':
                    eng.wait_ge(it[1], it[2])
                else:
                    name, a, kw = it[1]
                    ins = getattr(eng, name)(*a, **kw)
                    ins.then_inc(it[2], it[3])

        with nc.Block() as block:
            @block.tensor
            def _(e):
                replay(e, prog['pe'])

            @block.scalar
            def _(e):
                replay(e, prog['act'])

            @block.vector
            def _(e):
                replay(e, prog['dve'])

            @block.gpsimd
            def _(e):
                replay(e, prog['pool'])

            @block.sync
            def _(e):
                replay(e, prog['sp'])

    def barrier(self):
        for e in ('act', 'dve', 'sp'):
            w = self.waited[e]
            for e2, sem in self.esem.items():
                val = self.cnt[e2]
                if val > 0 and w.get(id(sem), 0) < val:
                    w[id(sem)] = val
                    self.prog[e].append(('w', sem, val))
            for dk, sem in self.dsem.items():
                if isinstance(dk, str) and dk.startswith('wb'):
                    continue
                val = self.dcnt[dk]
                if w.get(id(sem), 0) < val:
                    w[id(sem)] = val
                    self.prog[e].append(('w', sem, val))


S_LEN = 2048
D = 1024
NT = 16
IN_COLS = 6672
DFF = 2816
KFF = 22
EPS = 1e-6
C_GQ, C_GK, C_GV, C_LR, C_GR, C_PU, C_MQ, C_MK, C_MV, C_GA, C_GB, C_GC = (
    0, 256, 512, 1024, 1040, 1552, 2064, 2576, 3088, 3600, 4624, 5648)
POOL_W = (2, 4, 8, 16)
NEG = -30000.0
SB_BASE = 16512


class _Stop(Exception):
    pass


def build_program(layers, n_layers_total=2, debug=None, stop=None):
    marks = []

    def chk(name):
        marks.append((name, dict(S.cnt)))
        if stop == name:
            raise _Stop()

    nc = bass.Bass("TRN2", target_bir_lowering=False)
    L = n_layers_total

    def din(name, shape):
        return nc.dram_tensor(name, list(shape), F32, kind="ExternalInput").ap()

    x_in = din("x", [S_LEN, D])
    w_in = din("w_in", [L, D, IN_COLS])
    w_upa = din("w_up_a", [L, 512, D])
    w_upb = din("w_up_b", [L, 512, D])
    w_upc = din("w_up_c", [L, 512, D])
    w_out = din("w_out", [L, D, D])
    w_fg = din("ffn_w_gate", [L, D, DFF])
    w_fu = din("ffn_w_up", [L, D, DFF])
    w_fd = din("ffn_w_down", [L, DFF, D])
    pool_w = din("pool_w", [L, 4, 128, 128])
    wg2a = din("wg2a", [L, 17, 256])
    vecs = din("vecs", [L, 128, 24])
    rel_bias = din("rel_bias", [32, 4])
    c_ident = din("c_ident", [128, 128])
    c_tri = din("c_tri", [128, 256])
    c_bk = din("c_bk", [128, 384])
    c_negq = din("c_negq", [128, 8 * 32])
    c_pinv = din("c_pinv", [128, 64])
    y_out = nc.dram_tensor("y", [S_LEN, D], F32, kind="ExternalOutput").ap()
    x_mid = nc.dram_tensor("x_mid", [S_LEN, D], F32, kind="ExternalOutput").ap()
    dbg_out = None
    if debug:
        dbg_out = nc.dram_tensor("dbg", [128, 3, 4, S_LEN], F32, kind="ExternalOutput").ap()

    S = Sched(nc)
    ps_banks = [S.ps(f"psb{i}") for i in range(8)]
    ps_rr = {}

    def ps(lo=0, hi=8):
        n = hi - lo
        c = ps_rr.get((lo, hi), 0)
        ps_rr[(lo, hi)] = c + 1
        i = lo + c % n
        return ps_banks[i], f"ps{i}"

    identb = S.sb("identb", [128, 128], BF16)
    onesb = S.sb("onesb", [128, 128], BF16)
    tri = S.sb("tri", [128, 256], F32)
    trib = S.sb("trib", [128, 128], BF16)
    Tb = S.sb("Tb", [128, 4, 384], F32)
    relbc = S.sb("relbc", [128, 128], F32)
    negq = S.sb("negq", [128, 8, 32], F32)
    pinv = S.sb("pinv", [128, 4, 16], F32)
    vec = S.sb("vec", [128, 24], F32)
    epsc = S.sb("epsc", [128, 8], F32)
    pwb = S.sb("pwb", [128, 4, 128], BF16)
    ss = S.sb("ss", [128, 16], F32)
    rs = S.sb("rs", [128, 16], F32)
    rt = S.sb("rt", [128, 16], F32)
    sqj = [None]
    wg2 = S.sb("wg2", [17, 256], F32)
    NWB = 4
    wbs = [S.sb(f"wb{i}", [128, 4096], BF16) for i in range(NWB)]
    wb_rr = [0]
    REG0 = S.sb_off

    def R(name, shape, dtype, off):
        return S.sb(name + f"_{S.cnt['pe']}_{S.cnt['dve']}_{off}", shape, dtype, at=REG0 + off)

    KB = 1024
    hT = R("hT", [128, 8, S_LEN], BF16, 0)
    yT = [R(f"yT{i}", [128, 4, S_LEN], BF16, 32 * KB + i * 16 * KB) for i in range(3)]
    PH = 80 * KB
    assert REG0 + PH + 84 * KB <= S.sb_lim, (REG0, S.sb_lim)
    ctmp = R("ctmp", [128, 384], F32, PH + 14 * KB)

    S.dma('sp', tri[:], c_tri[:, :], writes=['tri'])
    S.dma('sp', ctmp[:, 0:128], c_ident[:, :], writes=['ctmp'])
    S.dma('sp', relbc[:], rel_bias.rearrange("b h -> (b h)").partition_broadcast(128), writes=['relbc'])
    S.dma('sp', negq[:].rearrange("p a b -> p (a b)"), c_negq[:, :], writes=['negq'])
    S.dma('sp', pinv[:].rearrange("p a b -> p (a b)"), c_pinv[:, :], writes=['pinv'])
    S.op('dve', lambda e: e.tensor_copy(out=identb[:], in_=ctmp[:, 0:128]), reads=['ctmp'], writes=['identb'])
    S.op('dve', lambda e: e.tensor_copy(out=trib[:], in_=tri[:, 0:128]), reads=['tri'], writes=['trib'])
    S.op('dve', lambda e: e.memset(onesb[:], 1.0), writes=['onesb'])
    S.op('dve', lambda e: e.memset(epsc[:], EPS), writes=['epsc'])
    def tb_closures(bkb, ctb):
        ops = []

        def init(h):
            return lambda: S.op('dve', lambda e: e.tensor_scalar(out=Tb[:, h, :], in0=bkb[:], scalar1=0.0, scalar2=NEG,
                                                                 op0=ALU.is_lt, op1=ALU.mult), reads=['bk'], writes=[('Tb', h)])

        def mask(b_):
            return lambda: S.op('dve', lambda e: e.tensor_scalar(out=ctb[:], in0=bkb[:], scalar1=float(b_), scalar2=None,
                                                                 op0=ALU.is_equal), reads=['bk'], writes=['ctmp2'])

        def fma(b_, h):
            return lambda: S.op('dve', lambda e: e.scalar_tensor_tensor(
                out=Tb[:, h, :], in0=ctb[:], scalar=relbc[:, b_ * 4 + h:b_ * 4 + h + 1], in1=Tb[:, h, :],
                op0=ALU.mult, op1=ALU.add), reads=['ctmp2', 'relbc', ('Tb', h)], writes=[('Tb', h)])
        for h in range(4):
            ops.append(init(h))
        for b_ in range(32):
            ops.append(mask(b_))
            for h in range(4):
                ops.append(fma(b_, h))
        return ops

    def load_wm(pieces, cast_eng=None):
        i = wb_rr[0] % NWB
        wb_rr[0] += 1
        wb = wbs[i]
        off = 0
        views = []
        for src_ap, K, ncols in pieces:
            n = K * ncols
            v = wb[:, off:off + n].rearrange("p (k c) -> p k c", k=K)
            S.dma('pool', v, src_ap.rearrange("(k p) c -> p k c", p=128), writes=[f'wb{i}'])
            views.append(v)
            off += n
        assert off <= 4096
        return views, f'wb{i}'

    def load_w(src_ap, K, ncols, cast_eng='pool'):
        views, key = load_wm([(src_ap, K, ncols)], cast_eng)
        return views[0], key

    def proj_fm(wb, wkey, K, ncols, srcT, srckey, evac, ts_list=(0, 1, 2, 3), pslo=0, pshi=8, defer=0):
        pend = []
        for m in range((ncols + 127) // 128):
            mc = min(128, ncols - m * 128)
            for ts in ts_list:
                pt, pk = ps(pslo, pshi)
                for k in range(K):
                    S.op('pe', lambda e: e.matmul(
                        pt[0:mc, 0:512], lhsT=wb[:, k, m * 128:m * 128 + mc], rhs=srcT[:, k, ts * 512:(ts + 1) * 512],
                        start=(k == 0), stop=(k == K - 1)), reads=[wkey, srckey], writes=[pk])
                pend.append((m, mc, ts, pt, pk))
                if len(pend) > defer:
                    evac(*pend.pop(0))
        while pend:
            evac(*pend.pop(0))

    def proj_tm(wb, wkey, K, ncols, srcT, srckey, evac, c0=0):
        for t in range(NT):
            pt, pk = ps()
            for k in range(K):
                S.op('pe', lambda e, pt=pt, k=k, t=t: e.matmul(
                    pt[:, 0:ncols], lhsT=srcT[:, k, t * 128:(t + 1) * 128], rhs=wb[:, k, c0:c0 + ncols],
                    start=(k == 0), stop=(k == K - 1)), reads=[wkey, srckey], writes=[pk])
            evac(t, pt, pk)

    def rstd_from_sum(ssum_ap, out_ap, tmp_ap, n, key_in, key_out, key_tmp):
        S.op('act', lambda e: e.activation(out=tmp_ap, in_=ssum_ap, func=AF.Ln, bias=epsc[:, 0:1], scale=1.0 / n),
             reads=[key_in, 'epsc'], writes=[key_tmp])
        S.op('act', lambda e: e.activation(out=out_ap, in_=tmp_ap, func=AF.Exp, scale=-0.5), reads=[key_tmp], writes=[key_out])

    def sumsq_tile(t, xt, xk, junk_ap):
        S.op('act', lambda e: e.activation(out=junk_ap, in_=xt, func=AF.Square, accum_out=ss[:, t:t + 1]),
             reads=[xk], writes=['junk', ('ss', t)])

    def norm_to_hT(l, get_tile, col0, NB=None, stats_ready=False):
        NB = PH if NB is None else NB
        junk = R("junk", [128, D], F32, NB + 1 * KB)
        xn = [R(f"xn{i}", [128, D], BF16, NB + 5 * KB + i * 2 * KB) for i in range(3)]
        tiles = []
        for t in range(NT):
            xt, xk = get_tile(t)
            tiles.append((xt, xk))
            if not stats_ready:
                sumsq_tile(t, xt, xk, junk[:])
        rstd_from_sum(ss[:], rs[:], rt[:], D, 'ss', 'rs', 'rt')
        for t in range(NT):
            xt, xk = tiles[t]
            xb = xn[t % 3]
            S.op('act', lambda e: e.mul(out=xb[:], in_=xt, mul=rs[:, t:t + 1]), reads=[xk, 'rs'], writes=[f'xn{t % 3}'])
            for half in range(2):
                pt, pk = ps()
                for kk in range(4):
                    k = half * 4 + kk
                    S.op('pe', lambda e: e.matmul(pt[:, kk * 128:(kk + 1) * 128], lhsT=xb[:, k * 128:(k + 1) * 128],
                                                  rhs=identb[:], start=True, stop=True),
                         reads=[f'xn{t % 3}', 'identb'], writes=[pk])
                for kk in range(4):
                    k = half * 4 + kk
                    S.op('dve', lambda e: e.tensor_scalar(
                        out=hT[:, k, t * 128:(t + 1) * 128], in0=pt[:, kk * 128:(kk + 1) * 128],
                        scalar1=vec[:, col0 + k:col0 + k + 1], scalar2=None, op0=ALU.mult),
                        reads=[pk, 'vec'], writes=[('hT', t, k)])

    xs_of = [None]

    def emit_layer(l, xin_ap, xout_ap, first):
        if not first:
            S.barrier()
        S.dma('sp', vec[:], vecs[l, :, :], writes=['vec'])
        S.dma('sp', wg2[:], wg2a[l, :, :], writes=['wg2'])
        WIN = w_in[l]

        if first:
            xall = R("xall", [128, NT, D], F32, PH + 18 * KB)

            def get_tile_dram(t):
                S.dma('sp', xall[:, t, :], xin_ap[t * 128:(t + 1) * 128, :], writes=[('xall', t)])
                return xall[:, t, :], ('xall', t)
            norm_to_hT(l, get_tile_dram, 0)
        else:
            xs_prev = xs_of[0]
            norm_to_hT(l, lambda t: (xs_prev[:, t, :], 'xs'), 0, PH + 48 * KB, stats_ready=True)
        S.barrier()
        chk('A')

        G0 = PH
        qTt = R("qTt", [128, 2, S_LEN], BF16, G0)
        kTz = R("kTz", [128, 2, 2, S_LEN], BF16, G0 + 8 * KB)
        ktl = R("ktl", [128, NT, 256], BF16, G0 + 24 * KB)
        E0 = G0 + 32 * KB
        vv = R("vv", [128, NT, 512], BF16, E0)
        expb = R("expb", [128, 2, S_LEN], F32, E0)
        expnb = R("expnb", [128, 2, S_LEN], F32, E0 + 16 * KB)
        etl = R("etl", [128, NT, 256], F32, E0 + 32 * KB)
        dec = R("dec", [128, 2, NT], F32, E0 + 48 * KB)
        Sst = R("Sst", [128, 2, 128], F32, E0 + 48 * KB + 256)
        Sbz = R("Sbz", [128, 2, 2, 128], BF16, E0 + 49 * KB + 256)
        lrT = R("lrT", [128, S_LEN], F32, G0)
        ez = [R(f"ez{i}", [128, 256], F32, G0 + 8 * KB + i * KB) for i in range(2)]
        lsp = [R(f"lsp{i}", [128, 256], F32, G0 + 10 * KB + i * KB) for i in range(2)]

        S.op('dve', lambda e: e.memset(lrT[0:32, :], 1.0), writes=['lrT'])
        wb, wk = load_w(WIN[:, C_LR:C_LR + 16], 8, 16)

        def ev_lr(m, mc, ts, pt, pk):
            S.op('act', lambda e: e.activation(out=lrT[0:16, ts * 512:(ts + 1) * 512], in_=pt[0:16, 0:512],
                                               func=AF.Copy), reads=[pk, 'lrT'], writes=[('lrTs', ts)])
        proj_fm(wb, wk, 8, 16, hT, 'hT', ev_lr)
        for t in range(NT):
            b = t % 2
            pz, pzk = ps()
            S.op('pe', lambda e, pz=pz, t=t: e.matmul(pz[:, 0:256], lhsT=lrT[0:17, t * 128:(t + 1) * 128],
                                                      rhs=wg2[0:17, :], start=True, stop=True),
                 reads=[('lrTs', t // 4), 'lrT', 'wg2'], writes=[pzk])
            S.op('act', lambda e, pz=pz, b=b: e.activation(out=ez[b][:], in_=pz[:, 0:256], func=AF.Exp, scale=-1.0),
                 reads=[pzk], writes=[f'ez{b}'])
            S.op('act', lambda e, b=b: e.activation(out=lsp[b][:], in_=ez[b][:], func=AF.Ln, bias=1.0),
                 reads=[f'ez{b}'], writes=[f'lsp{b}'])
            for c in range(2):
                pb, pbk = ps()
                S.op('pe', lambda e, pb=pb, b=b, c=c: e.matmul(pb[:, 0:128], lhsT=lsp[b][:, c * 128:(c + 1) * 128],
                                                               rhs=tri[:, 0:128], start=True, stop=True),
                     reads=[f'lsp{b}', 'tri'], writes=[pbk])
                S.op('act', lambda e, pb=pb, c=c, t=t: e.activation(out=expb[:, c, t * 128:(t + 1) * 128],
                                                                    in_=pb[:, 0:128], func=AF.Exp, scale=-1.0 / 16),
                     reads=[pbk], writes=[('expb', c, t)])
                S.op('act', lambda e, pb=pb, c=c, t=t: e.activation(out=expnb[:, c, t * 128:(t + 1) * 128],
                                                                    in_=pb[:, 0:128], func=AF.Exp, scale=1.0 / 16),
                     reads=[pbk], writes=[('expnb', c, t)])
                S.op('dve', lambda e, c=c, t=t: e.tensor_copy(out=dec[:, c, t:t + 1],
                                                              in_=expb[:, c, t * 128 + 127:t * 128 + 128]),
                     reads=[('expb', c, t)], writes=[('dec', c, t)])
            ptl, ptk = ps()
            S.op('pe', lambda e, ptl=ptl, b=b: e.matmul(ptl[:, 0:256], lhsT=tri[:, 128:256], rhs=lsp[b][:],
                                                        start=True, stop=True),
                 reads=[f'lsp{b}', 'tri'], writes=[ptk])
            S.op('act', lambda e, ptl=ptl, t=t: e.activation(out=etl[:, t, :], in_=ptl[:, 0:256], func=AF.Exp,
                                                             scale=-1.0 / 16), reads=[ptk], writes=[('etl', t)])
        S.barrier()

        chk('gla1')
        S.op('dve', lambda e: e.memset(kTz[:].rearrange("p a b c -> p (a b c)"), 0.0), writes=['kTz'])
        wb, wk = load_w(WIN[:, C_GQ:C_GQ + 256], 8, 256)

        def ev_q(m, mc, ts, pt, pk):
            S.op('dve', lambda e: e.scalar_tensor_tensor(
                out=qTt[:, m, ts * 512:(ts + 1) * 512], in0=pt[:, 0:512], scalar=0.125,
                in1=expb[:, m, ts * 512:(ts + 1) * 512], op0=ALU.mult, op1=ALU.mult),
                reads=[pk] + [('expb', m, ts * 4 + i) for i in range(4)], writes=[('qTt', m, ts)])
        proj_fm(wb, wk, 8, 256, hT, 'hT', ev_q)
        wb, wk = load_w(WIN[:, C_GK:C_GK + 256], 8, 256)

        def ev_k(m, mc, ts, pt, pk):
            for hh in range(2):
                r0, r1 = hh * 64, (hh + 1) * 64
                S.op('dve', lambda e, r0=r0, r1=r1, hh=hh: e.tensor_tensor(
                    out=kTz[r0:r1, m, hh, ts * 512:(ts + 1) * 512], in0=pt[r0:r1, 0:512],
                    in1=expnb[r0:r1, m, ts * 512:(ts + 1) * 512], op=ALU.mult),
                    reads=[pk, 'kTz'] + [('expnb', m, ts * 4 + i) for i in range(4)], writes=[('kTz', m, ts, hh)])
        proj_fm(wb, wk, 8, 256, hT, 'hT', ev_k)

        def ev_kt(t, pt, pk):
            S.op('dve', lambda e: e.tensor_tensor(out=ktl[:, t, :], in0=pt[:, 0:256], in1=etl[:, t, :], op=ALU.mult),
                 reads=[pk, ('etl', t)], writes=[('ktl', t)])
        proj_tm(wb, wk, 8, 256, hT, 'hT', ev_kt)
        S.barrier()
        wb, wk = load_w(WIN[:, C_GV:C_GV + 512], 8, 512)

        def ev_v(t, pt, pk):
            S.op('act', lambda e: e.activation(out=vv[:, t, :], in_=pt[:, 0:512], func=AF.Copy),
                 reads=[pk], writes=[('vv', t)])
        proj_tm(wb, wk, 8, 512, hT, 'hT', ev_v)
        S.barrier()

        chk('gla2')
        T0 = E0 + 16 * KB
        ATm = [R(f"ATm{i}", [128, 4, 128], BF16, T0 + i * KB) for i in range(2)]
        osq = [R(f"osq{i}", [128, 512], BF16, T0 + 2 * KB + i * KB) for i in range(3)]
        rsa = [R(f"rsa{i}", [128, 512], F32, T0 + 5 * KB + i * 2 * KB) for i in range(3)]
        rsb = rsa
        sil = [R(f"sil{i}", [128, 512], F32, PH + 80 * KB + i * 2 * KB) for i in range(2)]
        yaT = yT[0]
        S.op('dve', lambda e: e.memset(Sst[:].rearrange("p a b -> p (a b)"), 0.0), writes=['Sst'])
        S.op('dve', lambda e: e.memset(Sbz[:].rearrange("p a b c -> p (a b c)"), 0.0), writes=['Sbz'])
        po_of = {}

        def gla_pre(t):
            b = t % 2
            tok = slice(t * 128, (t + 1) * 128)
            pa, pak = ps(5, 7)
            for h in range(4):
                c, hh = h // 2, h % 2
                S.op('pe', lambda e: e.matmul(pa[:, h * 128:(h + 1) * 128], lhsT=kTz[:, c, hh, tok], rhs=qTt[:, c, tok],
                                              start=True, stop=True), reads=['kTz', 'qTt'], writes=[pak])
            for h in range(4):
                S.op('dve', lambda e: e.tensor_tensor(out=ATm[b][:, h, :], in0=pa[:, h * 128:(h + 1) * 128],
                                                      in1=tri[:, 0:128], op=ALU.mult),
                     reads=[pak, 'tri'], writes=[('ATm', b, h)])
            pd, pdk = ps(7, 8)
            for c in range(2):
                S.op('pe', lambda e: e.matmul(pd[:, c * 256:(c + 1) * 256], lhsT=ktl[:, t, c * 128:(c + 1) * 128],
                                              rhs=vv[:, t, c * 256:(c + 1) * 256], start=True, stop=True),
                     reads=['ktl', 'vv'], writes=[pdk])
            return pd, pdk

        def gla_o(t, pds):
            b = t % 2
            tok = slice(t * 128, (t + 1) * 128)
            po, pok = ps(0, 5)
            po_of[t] = (po, pok)
            for h in range(4):
                c, hh = h // 2, h % 2
                S.op('pe', lambda e: e.matmul(po[:, h * 128:(h + 1) * 128], lhsT=vv[:, t, h * 128:(h + 1) * 128],
                                              rhs=ATm[b][:, h, :], start=True, stop=False),
                     reads=['vv', ('ATm', b, h)], writes=[pok])
                S.op('pe', lambda e: e.matmul(po[:, h * 128:(h + 1) * 128], lhsT=Sbz[:, c, hh, :], rhs=qTt[:, c, tok],
                                              start=False, stop=True),
                     reads=[('Sbz', c, hh), 'Sbz', 'qTt'], writes=[pok])
            pd, pdk = pds
            for c in range(2):
                if t == NT - 1:
                    break
                for hh in range(2):
                    r0, r1 = hh * 64, (hh + 1) * 64
                    S.op('dve', lambda e: e.scalar_tensor_tensor(
                        out=Sst[r0:r1, c, :], in0=Sst[r0:r1, c, :], scalar=dec[r0:r1, c, t:t + 1],
                        in1=pd[r0:r1, c * 256 + hh * 128:c * 256 + (hh + 1) * 128], op0=ALU.mult, op1=ALU.add),
                        reads=[pdk, 'Sst', ('Sst', c, hh), 'dec'], writes=[('Sst', c, hh)])
                    S.op('act', lambda e: e.activation(out=Sbz[r0:r1, c, hh, :], in_=Sst[r0:r1, c, :], func=AF.Copy),
                         reads=[('Sst', c, hh), 'Sbz'], writes=[('Sbz', c, hh)])

        pss_of = {}

        def gla_n1(t):
            b = t % 3
            po, pok = po_of[t]
            S.op('act', lambda e: e.activation(out=osq[b][:], in_=po[:, 0:512], func=AF.Square), reads=[pok], writes=[f'osq{b}'])
            pss, pssk = ps(0, 5)
            pss_of[t] = (pss, pssk)
            S.op('pe', lambda e: e.matmul(pss[:, 0:512], lhsT=onesb[:], rhs=osq[b][:], start=True, stop=True),
                 reads=[f'osq{b}', 'onesb'], writes=[pssk])

        def gla_n2(t):
            b = t % 3
            tok = slice(t * 128, (t + 1) * 128)
            po, pok = po_of[t]
            pss, pssk = pss_of[t]
            rstd_from_sum(pss[:, 0:512], rsb[b][:], rsa[b][:], 128, pssk, f'rsa{b}', f'rsa{b}')
            S.op('dve', lambda e: e.scalar_tensor_tensor(
                out=yaT[:, :, tok], in0=po[:, 0:512].rearrange("p (h q) -> p h q", h=4), scalar=vec[:, 16:17],
                in1=rsb[b][:].rearrange("p (h q) -> p h q", h=4), op0=ALU.mult, op1=ALU.mult),
                reads=[pok, f'rsa{b}', 'vec'], writes=[('yaT', t)])

        for t in range(NT + 2):
            pds = gla_pre(t) if t < NT else None
            if 0 <= t - 1 < NT:
                gla_n1(t - 1)
            if 0 <= t - 2 < NT:
                gla_n2(t - 2)
            if t < NT:
                gla_o(t, pds)
        chk('gla3')
        puT = R("puT", [128, 4, S_LEN], F32, PH)
        pa_ = R("ppa", [128, S_LEN], F32, PH + 32 * KB)
        pb_ = R("ppb", [128, S_LEN], F32, PH + 40 * KB)
        ppb16 = R("ppb16", [128, 4, S_LEN], BF16, PH + 64 * KB)
        ybT = yT[1]
        S.dma('pool', pwb[:], pool_w[l].rearrange("g c e -> c g e"), writes=['pwb'])
        wb, wk = load_w(WIN[:, C_PU:C_PU + 512], 8, 512)

        def ev_pu(m, mc, ts, pt, pk):
            S.op('act', lambda e: e.activation(out=puT[:, m, ts * 512:(ts + 1) * 512], in_=pt[:, 0:512], func=AF.Copy),
                 reads=[pk], writes=[('puT', m)])
        proj_fm(wb, wk, 8, 512, hT, 'hT', ev_pu)
        pool_ops = []

        def pool_group(g):
            w = POOL_W[g]
            st8 = {'src': puT[:, g, :], 'sk': ('puT', g), 'bi': 0}
            bufs = [(pa_, 'ppa'), (pb_, 'ppb')]

            def shift(sh):
                def f():
                    dst, dk = bufs[st8['bi']]
                    st8['bi'] ^= 1
                    src, sk = st8['src'], st8['sk']
                    S.op('dve', lambda e: e.tensor_copy(out=dst[:, 0:sh], in_=src[:, 0:sh]), reads=[sk], writes=[dk])
                    S.op('dve', lambda e: e.tensor_tensor(out=dst[:, sh:S_LEN], in0=src[:, sh:S_LEN], in1=src[:, 0:S_LEN - sh],
                                                          op=ALU.add), reads=[sk, dk], writes=[dk])
                    st8['src'], st8['sk'] = dst[:], dk
                return f
            sh = 1
            while sh < w:
                pool_ops.append(shift(sh))
                sh *= 2

            def scale():
                dst, dk = bufs[st8['bi']]
                src, sk = st8['src'], st8['sk']
                S.op('dve', lambda e: e.scalar_tensor_tensor(
                    out=dst[:, 16:S_LEN], in0=src[:, 16:S_LEN], scalar=1.0 / w, in1=puT[:, g, 16:S_LEN],
                    op0=ALU.mult, op1=ALU.subtract), reads=[sk, ('puT', g), dk], writes=[dk])
                S.op('dve', lambda e: e.tensor_tensor(out=dst[:, 0:16], in0=src[:, 0:16], in1=pinv[:, g, :], op=ALU.mult),
                     reads=[sk, 'pinv', dk], writes=[dk])
                S.op('dve', lambda e: e.tensor_tensor(out=dst[:, 0:16], in0=dst[:, 0:16], in1=puT[:, g, 0:16], op=ALU.subtract),
                     reads=[dk, ('puT', g)], writes=[dk])
                S.op('act', lambda e: e.activation(out=ppb16[:, g, :], in_=dst[:], func=AF.Copy),
                     reads=[dk], writes=[('ppb16', g)])
            pool_ops.append(scale)

            def mm(ts):
                def f():
                    pt, pk = ps()
                    S.op('pe', lambda e: e.matmul(pt[:, 0:512], lhsT=pwb[:, g, :], rhs=ppb16[:, g, ts * 512:(ts + 1) * 512],
                                                  start=True, stop=True), reads=['pwb', ('ppb16', g)], writes=[pk])
                    S.op('dve', lambda e: e.tensor_scalar(
                        out=ybT[:, g, ts * 512:(ts + 1) * 512], in0=pt[:, 0:512], scalar1=vec[:, 17 + g:18 + g],
                        scalar2=None, op0=ALU.mult), reads=[pk, 'vec'], writes=[('ybT', g, ts)])
                return f
            for ts in range(4):
                pool_ops.append(mm(ts))

        for g in range(4):
            pool_group(g)
        chk('gla4')
        wb, wk = load_w(WIN[:, C_GR:C_GR + 512], 8, 512)

        def ev_r(m, mc, ts, pt, pk):
            b = (m * 4 + ts) % 2
            S.op('act', lambda e: e.activation(out=sil[b][:], in_=pt[:, 0:512], func=AF.Silu),
                 reads=[pk], writes=[f'sil{b}'])
            S.op('dve', lambda e: e.tensor_tensor(out=yaT[:, m, ts * 512:(ts + 1) * 512],
                                                  in0=yaT[:, m, ts * 512:(ts + 1) * 512], in1=sil[b][:], op=ALU.mult),
                 reads=[f'sil{b}'] + [('yaT', ts * 4 + i) for i in range(4)], writes=[('yaTf', m, ts)])
            for _ in range(2):
                if pool_ops:
                    pool_ops.pop(0)()
        proj_fm(wb, wk, 8, 512, hT, 'hT', ev_r)
        while pool_ops:
            pool_ops.pop(0)()
        S.barrier()
        chk('pool')
        emit_moba(l, WIN, first)
        S.barrier()
        chk('moba')
        emit_merge_ffn(l, WIN, xin_ap, xout_ap, xout_ap is y_out)

    def emit_moba(l, WIN, first=False):
        M0 = PH
        qhT = R("qhT", [128, 4, S_LEN], BF16, M0)
        khT = R("khT", [128, 4, S_LEN], BF16, M0 + 16 * KB)
        v1 = R("v1", [128, NT, 4, 130], BF16, M0 + 32 * KB)
        X0 = M0 + 49 * KB
        NPB = 4
        sq = [R(f"msq{i}", [128, 512], BF16, X0 + i * KB) for i in range(NPB)]
        mra = [R(f"mra{i}", [128, 512], F32, X0 + 4 * KB + i * 2 * KB) for i in range(NPB)]
        mrb = [R(f"mrb{i}", [128, 512], F32, X0 + 12 * KB + i * 2 * KB) for i in range(NPB)]
        kmf = R("kmf", [128, 4, 8], F32, X0)
        kmb = R("kmb", [128, 4, 8], BF16, X0 + 128)
        msk = R("msk", [128, 32], F32, X0 + 256)
        top8 = R("top8", [128, 4, 8], F32, X0 + 384)
        sel = R("sel", [128, NT, 32], F32, X0 + 1 * KB)
        NPT = 6
        PTb = [R(f"PTb{i}", [128, 256], BF16, X0 + 3 * KB + i * 512) for i in range(NPT)]
        ltm = [R(f"ltm{i}", [128, 256], F32, X0 + 6 * KB + i * KB) for i in range(4)]
        acc = [[R(f"acc{a_}{b_}", [128, 4, 132], F32, X0 + 10 * KB + (a_ * 2 + b_) * 2112) for b_ in range(2)] for a_ in range(2)]
        rec = R("rec", [128, 2, 4], F32, X0 + 19 * KB)
        ycn = [R(f"ycn{i}", [128, 512], BF16, X0 + 20 * KB + i * KB) for i in range(2)]
        ycT = yT[2]
        scale = 128.0 ** -0.5

        wq, wqk = load_w(WIN[:, C_MQ:C_MQ + 512], 8, 512)
        wk_, wkk = load_w(WIN[:, C_MK:C_MK + 512], 8, 512)
        groups = []
        for (wb, wkey, dstT, wcol, name) in ((wq, wqk, qhT, 21, 'qhT'), (wk_, wkk, khT, 22, 'khT')):
            for m in range(4):
                for ts in range(4):
                    groups.append(dict(wb=wb, wkey=wkey, dstT=dstT, wcol=wcol, name=name, m=m, ts=ts))
        NG = len(groups)

        def st_mm(i):
            g = groups[i]
            pt, pk = ps(0, 4)
            g['pt'], g['pk'] = pt, pk
            for k in range(8):
                S.op('pe', lambda e: e.matmul(pt[:, 0:512], lhsT=g['wb'][:, k, g['m'] * 128:(g['m'] + 1) * 128],
                                              rhs=hT[:, k, g['ts'] * 512:(g['ts'] + 1) * 512], start=(k == 0), stop=(k == 7)),
                     reads=[g['wkey'], 'hT'], writes=[pk])

        def st_sq(i):
            g = groups[i]
            b_ = i % NPB
            S.op('act', lambda e: e.activation(out=sq[b_][:], in_=g['pt'][:, 0:512], func=AF.Square),
                 reads=[g['pk']], writes=[f'msq{b_}'])
            pss, pssk = ps(4, 8)
            g['pss'], g['pssk'] = pss, pssk
            S.op('pe', lambda e: e.matmul(pss[:, 0:512], lhsT=onesb[:], rhs=sq[b_][:], start=True, stop=True),
                 reads=[f'msq{b_}', 'onesb'], writes=[pssk])

        def st_rt(i):
            g = groups[i]
            b_ = i % NPB
            S.op('act', lambda e: e.activation(out=mra[b_][:], in_=g['pss'][:, 0:512], func=AF.Ln, bias=epsc[:, 0:1],
                                               scale=1.0 / 128), reads=[g['pssk'], 'epsc'], writes=[f'mra{b_}'])
            S.op('act', lambda e: e.activation(out=mrb[b_][:], in_=mra[b_][:], func=AF.Exp, scale=-0.5),
                 reads=[f'mra{b_}'], writes=[f'mrb{b_}'])

        def st_fin(i):
            g = groups[i]
            b_ = i % NPB
            S.op('dve', lambda e: e.scalar_tensor_tensor(
                out=g['dstT'][:, g['m'], g['ts'] * 512:(g['ts'] + 1) * 512], in0=g['pt'][:, 0:512],
                scalar=vec[:, g['wcol']:g['wcol'] + 1], in1=mrb[b_][:], op0=ALU.mult, op1=ALU.mult),
                reads=[g['pk'], f'mrb{b_}', 'vec'], writes=[(g['name'], g['m'], g['ts'])])

        tb_ops = []
        if first:
            bk2 = R("bk2", [128, 384], F32, X0 + 24 * KB)
            ct2 = R("ct2", [128, 384], F32, X0 + 26 * KB)
            S.dma('sp', bk2[:], c_bk[:, :], writes=['bk'])
            tb_ops = tb_closures(bk2, ct2)
        for i in range(NG + 3):
            for _ in range(6):
                if tb_ops:
                    tb_ops.pop(0)()
            if i < NG:
                st_mm(i)
            if 0 <= i - 1 < NG:
                st_sq(i - 1)
            if 0 <= i - 2 < NG:
                st_rt(i - 2)
            if 0 <= i - 3 < NG:
                st_fin(i - 3)
        while tb_ops:
            tb_ops.pop(0)()
        S.op('dve', lambda e: e.memset(v1[:].rearrange("p a b c -> p (a b c)"), 1.0), writes=['v1'])
        wb, wk = load_w(WIN[:, C_MV:C_MV + 512], 8, 512)

        def ev_mv(t, pt, pk):
            S.op('act', lambda e: e.activation(out=v1[:, t, :, 0:128],
                                               in_=pt[:, 0:512].rearrange("p (h d) -> p h d", h=4), func=AF.Copy),
                 reads=[pk, 'v1'], writes=[('v1', t)])
        proj_tm(wb, wk, 8, 512, hT, 'hT', ev_mv)
        S.barrier()
        chk('moba_proj')
        for h in range(4):
            S.op('dve', lambda e, h=h: e.tensor_reduce(out=kmf[:, h, :],
                                                       in_=khT[:, h, :].rearrange("p (n j) -> p n j", n=8),
                                                       axis=AX.X, op=ALU.add), reads=['khT'], writes=[('kmf', h)])
        S.op('dve', lambda e: e.tensor_scalar(out=kmb[:].rearrange("p a b -> p (a b)"),
                                              in0=kmf[:].rearrange("p a b -> p (a b)"), scalar1=1.0 / 256,
                                              scalar2=None, op0=ALU.mult),
             reads=[('kmf', h) for h in range(4)], writes=['kmb'])
        for i in range(2, NT):
            qb = i // 2
            pr, prk = ps(4, 8)
            for h in range(4):
                S.op('pe', lambda e, pr=pr, h=h, i=i: e.matmul(pr[:, h * 8:(h + 1) * 8], lhsT=qhT[:, h, i * 128:(i + 1) * 128],
                                                              rhs=kmb[:, h, :], start=True, stop=True),
                     reads=['kmb', 'qhT'], writes=[prk])
            S.op('dve', lambda e, pr=pr, qb=qb: e.tensor_tensor(out=msk[:], in0=pr[:, 0:32], in1=negq[:, qb, :], op=ALU.add),
                 reads=[prk, 'negq'], writes=['msk'])
            for h in range(4):
                S.op('dve', lambda e, h=h: e.max(out=top8[:, h, :], in_=msk[:, h * 8:(h + 1) * 8]),
                     reads=['msk'], writes=[('top8', h)])
            for h in range(4):
                S.op('dve', lambda e, h=h, i=i: e.tensor_scalar(out=sel[:, i, h * 8:(h + 1) * 8], in0=msk[:, h * 8:(h + 1) * 8],
                                                                scalar1=top8[:, h, 2:3], scalar2=None, op0=ALU.is_ge),
                     reads=['msk', ('top8', h)], writes=[('sel', i)])
        chk('moba_route')
        pt_rr = [0]

        def stage1(tk):
            h, jj, ncol, q0, bias_ap = tk['h'], tk['jj'], tk['ncol'], tk['q0'], tk['bias']
            st, stk = ps(4, 8)
            S.op('pe', lambda e: e.matmul(st[:, 0:ncol], lhsT=khT[:, h, jj * 128:(jj + 1) * 128],
                                          rhs=qhT[:, h, q0:q0 + ncol], start=True, stop=True),
                 reads=['qhT', 'khT'], writes=[stk])
            pi = pt_rr[0] % NPT
            pt_rr[0] += 1
            P = PTb[pi]
            if bias_ap is None:
                S.op('act', lambda e: e.activation(out=P[:, 0:ncol], in_=st[:, 0:ncol], func=AF.Exp,
                                                   bias=relbc[:, 124 + h:125 + h], scale=scale),
                     reads=[stk, 'relbc'], writes=[f'PTb{pi}'])
            else:
                li = pt_rr[0] % 4
                S.op('dve', lambda e: e.scalar_tensor_tensor(out=ltm[li][:, 0:ncol], in0=st[:, 0:ncol], scalar=scale,
                                                             in1=bias_ap, op0=ALU.mult, op1=ALU.add),
                     reads=[stk, ('Tb', h)], writes=[f'ltm{li}'])
                S.op('act', lambda e: e.activation(out=P[:, 0:ncol], in_=ltm[li][:, 0:ncol], func=AF.Exp),
                     reads=[f'ltm{li}'], writes=[f'PTb{pi}'])
            tk['P'], tk['Pk'] = P, f'PTb{pi}'

        def stage2(tk):
            h, jj, P, Pk = tk['h'], tk['jj'], tk['P'], tk['Pk']
            for grp, cb, st_, sp_ in tk['pv']:
                if grp['bank'] is None:
                    grp['bank'] = ps(0, 4)
                pb, pbk = grp['bank']
                S.op('pe', lambda e: e.matmul(pb[:, 0:129], lhsT=P[:, cb * 128:(cb + 1) * 128], rhs=v1[:, jj, h, 0:129],
                                              start=st_, stop=sp_), reads=[Pk, 'v1'], writes=[pbk])
            for po_ in tk['post']:
                if po_[0] == 'copy':
                    _, grp, qi, par = po_
                    pb, pbk = grp['bank']
                    S.op('dve', lambda e: e.tensor_copy(out=acc[qi][par][:, h, 0:129], in_=pb[:, 0:129]),
                         reads=[pbk], writes=[('acc', qi, par, h)])
                elif po_[0] == 'acc':
                    _, grp, qi, par, n, ii = po_
                    pb, pbk = grp['bank']
                    S.op('dve', lambda e: e.scalar_tensor_tensor(
                        out=acc[qi][par][:, h, 0:129], in0=pb[:, 0:129], scalar=sel[:, ii, h * 8 + n:h * 8 + n + 1],
                        in1=acc[qi][par][:, h, 0:129], op0=ALU.mult, op1=ALU.add),
                        reads=[pbk, ('sel', ii), ('acc', qi, par, h)], writes=[('acc', qi, par, h)])
                else:
                    finalize(po_[1])

        def finalize(QB):
            par = QB % 2
            for qi in range(2):
                ii = 2 * QB + qi
                A = acc[qi][par]
                S.op('dve', lambda e: e.reciprocal(out=rec[:, qi, :], in_=A[:, :, 128]),
                     reads=[('acc', qi, par, h) for h in range(4)], writes=[('rec', qi)])
                for h in range(4):
                    S.op('dve', lambda e: e.tensor_scalar(
                        out=ycn[qi][:, h * 128:(h + 1) * 128], in0=A[:, h, 0:128], scalar1=rec[:, qi, h:h + 1],
                        scalar2=None, op0=ALU.mult), reads=[('rec', qi), ('acc', qi, par, h)], writes=[('ycn', qi, h)])
                ptp, ptk = ps(4, 8)
                for h in range(4):
                    S.op('pe', lambda e: e.matmul(ptp[:, h * 128:(h + 1) * 128], lhsT=ycn[qi][:, h * 128:(h + 1) * 128],
                                                  rhs=identb[:], start=True, stop=True),
                         reads=[('ycn', qi, h), 'identb'], writes=[ptk])
                S.op('act', lambda e: e.activation(out=ycT[:, :, ii * 128:(ii + 1) * 128],
                                                   in_=ptp[:, 0:512].rearrange("p (h q) -> p h q", h=4), func=AF.Copy),
                     reads=[ptk], writes=[('ycT', ii)])

        tasks = []
        for QB in range(8):
            par = QB % 2
            i0, i1 = 2 * QB, 2 * QB + 1
            for h in range(4):
                g0, g1 = {'bank': None}, {'bank': None}
                tasks.append(dict(h=h, jj=i0, ncol=256, q0=QB * 256, bias=Tb[:, h, 0:256],
                                  pv=[(g0, 0, True, True), (g1, 1, True, False)], post=[('copy', g0, 0, par)]))
                tasks.append(dict(h=h, jj=i1, ncol=128, q0=i1 * 128, bias=Tb[:, h, 0:128],
                                  pv=[(g1, 0, False, True)], post=[('copy', g1, 1, par)]))
                for n in range(QB):
                    a0, a1 = {'bank': None}, {'bank': None}
                    for jj in (2 * n, 2 * n + 1):
                        first, last = (jj == 2 * n), (jj == 2 * n + 1)
                        tasks.append(dict(h=h, jj=jj, ncol=256, q0=QB * 256,
                                          bias=(Tb[:, h, 128:384] if jj == 2 * QB - 1 else None),
                                          pv=[(a0, 0, first, last), (a1, 1, first, last)],
                                          post=([('acc', a0, 0, par, n, i0), ('acc', a1, 1, par, n, i1)] if last else [])))
            tasks[-1]['post'] = list(tasks[-1]['post']) + [('fin', QB)]
        DEPTH = 3
        for i in range(len(tasks) + DEPTH):
            if i < len(tasks):
                stage1(tasks[i])
            if i >= DEPTH:
                stage2(tasks[i - DEPTH])

    def emit_merge_ffn(l, WIN, xin_ap, xout_ap, is_final):
        mT = R("mT", [128, 8, S_LEN], BF16, PH + 16 * KB)
        sg = [R(f"sg{i}", [128, 512], F32, PH + 48 * KB + i * 2 * KB) for i in range(3)]
        pr_ = [R(f"pr{i}", [128, 512], F32, PH + 54 * KB + i * 2 * KB) for i in range(3)]
        ups = [w_upa[l], w_upb[l], w_upc[l]]
        gcols = [C_GA, C_GB, C_GC]

        def merge_step(m, ts, gv, gk, uv, uk):
            pgs, pus = [], []
            for x in range(3):
                pg, pgk = ps()
                for k in range(8):
                    S.op('pe', lambda e: e.matmul(pg[:, 0:512], lhsT=gv[x][:, k, :], rhs=hT[:, k, ts * 512:(ts + 1) * 512],
                                                  start=(k == 0), stop=(k == 7)), reads=[gk, 'hT'], writes=[pgk])
                pgs.append((pg, pgk))
            for x in range(3):
                pu_, puk = ps()
                for k in range(4):
                    S.op('pe', lambda e: e.matmul(pu_[:, 0:512], lhsT=uv[x][:, k, :], rhs=yT[x][:, k, ts * 512:(ts + 1) * 512],
                                                  start=(k == 0), stop=(k == 3)), reads=[uk, ('yaTf', 'ybT', 'ycT')[x]], writes=[puk])
                pus.append((pu_, puk))
            for x in range(3):
                S.op('act', lambda e: e.activation(out=sg[x][:], in_=pgs[x][0][:, 0:512], func=AF.Sigmoid),
                     reads=[pgs[x][1]], writes=[f'sg{x}'])
                S.op('dve', lambda e: e.tensor_tensor(out=pr_[x][:], in0=pus[x][0][:, 0:512], in1=sg[x][:], op=ALU.mult),
                     reads=[pus[x][1], f'sg{x}'], writes=[f'pr{x}'])
            S.op('dve', lambda e: e.tensor_tensor(out=pr_[0][:], in0=pr_[0][:], in1=pr_[1][:], op=ALU.add),
                 reads=['pr0', 'pr1'], writes=['pr0'])
            S.op('dve', lambda e: e.tensor_tensor(out=mT[:, m, ts * 512:(ts + 1) * 512], in0=pr_[0][:], in1=pr_[2][:],
                                                   op=ALU.add), reads=['pr0', 'pr2'], writes=[('mT', m, ts)])

        for m in range(8):
            gv, gk = load_wm([(WIN[:, gcols[x] + m * 128:gcols[x] + (m + 1) * 128], 8, 128) for x in range(3)])
            uv, uk = load_wm([(ups[x][:, m * 128:(m + 1) * 128], 4, 128) for x in range(3)])
            for ts in range(4):
                merge_step(m, ts, gv, gk, uv, uk)
        S.barrier()
        chk('D')
        xs = S.sb(f"xs_{l}", [128, NT, D], F32, at=REG0 + 32 * KB)
        xld = [R(f"xld{i}", [128, D], F32, PH + 48 * KB + i * 4 * KB) for i in range(2)]
        sqjunk = R("sqjunk", [128, D], F32, PH + 56 * KB)
        xs_of[0] = xs
        wo = [load_w(w_out[l][:, hf * 512:(hf + 1) * 512], 8, 512) for hf in range(2)]

        def out_tile(t):
            b = t % 2
            S.dma('sp', xld[b][:], xin_ap[t * 128:(t + 1) * 128, :], writes=[f'xld{b}'])
            for hf in range(2):
                pt, pk = ps()
                for k in range(8):
                    S.op('pe', lambda e: e.matmul(pt[:, 0:512], lhsT=mT[:, k, t * 128:(t + 1) * 128], rhs=wo[hf][0][:, k, :],
                                                  start=(k == 0), stop=(k == 7)), reads=[wo[hf][1], 'mT'], writes=[pk])
                S.op('dve', lambda e: e.tensor_tensor(out=xs[:, t, hf * 512:(hf + 1) * 512], in0=pt[:, 0:512],
                                                      in1=xld[b][:, hf * 512:(hf + 1) * 512], op=ALU.add),
                     reads=[pk, f'xld{b}'], writes=[('xs', t, hf)])
            sumsq_tile(t, xs[:, t, :], 'xs', sqjunk[:])
        for t in range(NT):
            out_tile(t)
        S.barrier()

        chk('E')
        def get_tile_sb(t):
            return xs[:, t, :], 'xs'
        norm_to_hT(l, get_tile_sb, 8, PH + 48 * KB, stats_ready=True)
        S.barrier()
        chk('F')
        aT = [R(f"aT{i}", [128, 4, S_LEN], BF16, PH + 16 * KB + i * 16 * KB) for i in range(2)]
        st_ = [R(f"fs{i}", [128, 512], F32, PH + 48 * KB + i * 2 * KB) for i in range(3)]
        gjunk = R("gjunk", [128, D], F32, PH + 56 * KB)
        groups = [(g * 4, min(4, KFF - g * 4)) for g in range((KFF + 3) // 4)]

        def ffn_gu(gi, m, ts, A, gv, gk, uv, uk):
            pg, pgk = ps()
            pu_, puk = ps()
            for k in range(8):
                S.op('pe', lambda e: e.matmul(pg[:, 0:512], lhsT=gv[:, k, m * 128:(m + 1) * 128], rhs=hT[:, k, ts * 512:(ts + 1) * 512],
                                              start=(k == 0), stop=(k == 7)), reads=[gk, 'hT'], writes=[pgk])
            for k in range(8):
                S.op('pe', lambda e: e.matmul(pu_[:, 0:512], lhsT=uv[:, k, m * 128:(m + 1) * 128], rhs=hT[:, k, ts * 512:(ts + 1) * 512],
                                              start=(k == 0), stop=(k == 7)), reads=[uk, 'hT'], writes=[puk])
            si = (m * 4 + ts) % 3
            S.op('act', lambda e: e.activation(out=st_[si][:], in_=pg[:, 0:512], func=AF.Silu), reads=[pgk], writes=[f'fs{si}'])
            S.op('dve', lambda e: e.tensor_tensor(out=A[:, m, ts * 512:(ts + 1) * 512], in0=pu_[:, 0:512], in1=st_[si][:], op=ALU.mult),
                 reads=[puk, f'fs{si}'], writes=[('aT', gi % 2, m, ts)])

        def ffn_down(gi, t, A, kn, dv, dk, last):
            for hf in range(2):
                pt, pk = ps()
                for k in range(kn):
                    S.op('pe', lambda e: e.matmul(pt[:, 0:512], lhsT=A[:, k, t * 128:(t + 1) * 128],
                                                  rhs=dv[:, k, hf * 512:(hf + 1) * 512], start=(k == 0), stop=(k == kn - 1)),
                         reads=[dk, ('aT', gi % 2, k, t // 4)], writes=[pk])
                S.op('dve', lambda e: e.tensor_tensor(out=xs[:, t, hf * 512:(hf + 1) * 512], in0=xs[:, t, hf * 512:(hf + 1) * 512],
                                                      in1=pt[:, 0:512], op=ALU.add),
                     reads=[pk, ('xs', t, hf)], writes=[('xs', t, hf)])
            if last:
                S.dma('sp', xout_ap[t * 128:(t + 1) * 128, :], xs[:, t, :], reads=[('xs', t, 0), ('xs', t, 1)], writes=['xout'])
                if not is_final:
                    sumsq_tile(t, xs[:, t, :], 'xs', gjunk[:])

        for gi, (k0, kn) in enumerate(groups):
            ncl = kn * 128
            A = aT[gi % 2]
            gv, gk = load_w(w_fg[l][:, k0 * 128:k0 * 128 + ncl], 8, ncl)
            uv, uk = load_w(w_fu[l][:, k0 * 128:k0 * 128 + ncl], 8, ncl)
            for m in range(kn):
                for ts in range(4):
                    ffn_gu(gi, m, ts, A, gv, gk, uv, uk)
            dv, dk = load_w(w_fd[l][k0 * 128:k0 * 128 + ncl, :], kn, 1024)
            for t in range(NT):
                ffn_down(gi, t, A, kn, dv, dk, gi == len(groups) - 1)

    n = len(layers)
    try:
        for idx, l in enumerate(layers):
            xin_ap = x_in if idx == 0 else x_mid
            xout_ap = y_out if idx == n - 1 else x_mid
            emit_layer(l, xin_ap, xout_ap, idx == 0)
    except _Stop:
        pass
    if 'xout' not in S.dsem:
        S.barrier()
        S.dma('sp', y_out[0:128, 0:256], tri[:, 0:256], writes=['xout'])
    if debug:
        S.barrier()
        dtmp = R("dtmp", [128, 4, S_LEN], F32, PH)
        srcs = [yT[0][:], yT[1][:], yT[2][:]] if stop != 'A' else [hT[:, 0:4, :], hT[:, 4:8, :], yT[2][:]]
        for i in range(3):
            S.op('dve', lambda e, i=i: e.tensor_copy(out=dtmp[:], in_=srcs[i]),
                 reads=['dbgx'], writes=['dtmp'])
            S.dma('sp', dbg_out[:, i, :, :], dtmp[:], reads=['dtmp'], writes=['dbgx'])
        S.finish(['xout', 'dbgx'])
    else:
        S.finish(['xout'])
    S.marks = marks
    return nc, S


def t5_bucket_np(dist):
    n = np.maximum(dist, 0)
    large = 16 + (np.log(np.maximum(n, 1).astype(np.float32) / np.float32(16)) / np.float32(np.log(128 / 16))
                  * np.float32(16)).astype(np.int32)
    large = np.minimum(large, 31)
    return np.where(n < 16, n, large)


def host_constants():
    c = {}
    c["c_ident"] = np.eye(128, dtype=np.float32)
    j = np.arange(128)[:, None]
    i = np.arange(128)[None, :]
    c["c_tri"] = np.concatenate([(j <= i), (j > i)], axis=1).astype(np.float32)
    k = np.arange(128)[:, None]
    m = np.arange(384)[None, :]
    dist = m - k
    c["c_bk"] = np.where(dist >= 0, t5_bucket_np(dist), -1).astype(np.float32)
    nq = np.zeros((8, 4, 8), np.float32)
    for qb in range(8):
        nq[qb, :, qb:] = -1e30
    c["c_negq"] = np.ascontiguousarray(np.broadcast_to(nq.reshape(1, 256), (128, 256))).astype(np.float32)
    pv = np.zeros((4, 16), np.float32)
    for g, w in enumerate(POOL_W):
        pv[g] = 1.0 / np.minimum(np.arange(16) + 1, w)
    c["c_pinv"] = np.ascontiguousarray(np.broadcast_to(pv.reshape(1, 64), (128, 64))).astype(np.float32)
    return c


def pack_small(inputs):
    L = inputs["norm1_w"].shape[0]
    vecs = np.zeros((L, 128, 24), np.float32)
    for l in range(L):
        vecs[l, :, 0:8] = np.asarray(inputs["norm1_w"][l]).reshape(8, 128).T
        vecs[l, :, 8:16] = np.asarray(inputs["norm2_w"][l]).reshape(8, 128).T
        vecs[l, :, 16] = np.asarray(inputs["gla_norm_w"][l])
        vecs[l, :, 17:21] = np.asarray(inputs["pool_scale"][l]).reshape(4, 128).T
        vecs[l, :, 21] = np.asarray(inputs["moba_qn_w"][l])
        vecs[l, :, 22] = np.asarray(inputs["moba_kn_w"][l])
    wg2a = np.concatenate([np.asarray(inputs["gla_wg2"]), np.asarray(inputs["gla_bg"])[:, None, :]], axis=1)
    return vecs, np.ascontiguousarray(wg2a.astype(np.float32))


_CACHE = {}


def make_in_maps(inputs, x_full):
    vecs, wg2a = pack_small(inputs)
    consts = host_constants()
    shared = {k: np.ascontiguousarray(np.asarray(inputs[k], dtype=np.float32)) for k in
              ("w_in", "w_up_a", "w_up_b", "w_up_c", "w_out", "ffn_w_gate", "ffn_w_up", "ffn_w_down", "pool_w",
               "rel_bias")}
    shared["vecs"] = vecs
    shared["wg2a"] = wg2a
    shared.update(consts)
    maps = []
    for c in range(8):
        m = dict(shared)
        m["x"] = np.ascontiguousarray(x_full[c])
        maps.append(m)
    return maps


FUSED = True


def kernel(**inputs):
    x = np.asarray(inputs["x"], dtype=np.float32)
    if FUSED:
        if "fused" not in _CACHE:
            _CACHE["fused"] = build_program([0, 1])[0]
        res = run_bass_kernel_spmd(_CACHE["fused"], make_in_maps(inputs, x), core_ids=list(range(8)))
        return np.stack([np.asarray(r["y"]) for r in res.results], axis=0).astype(np.float32)
    for l in range(2):
        if l not in _CACHE:
            _CACHE[l] = build_program([l])[0]
        res = run_bass_kernel_spmd(_CACHE[l], make_in_maps(inputs, x), core_ids=list(range(8)))
        x = np.stack([np.asarray(r["y"]) for r in res.results], axis=0).astype(np.float32)
    return x
```
